# Optimizing a Trainium2 kernel written in Bass

```python
import math
import jax, jax.numpy as jnp
from jax import lax
import numpy as np

D_MODEL = 2048
BATCH = 4
SEQ = 4096
DEPTH = 1

MIX_WIDTH = D_MODEL
MLSTM_HEADS = 4
MLSTM_WIDTH = MIX_WIDTH // 2
MLSTM_HEAD_DIM = MLSTM_WIDTH // MLSTM_HEADS
MLSTM_CHUNK = 64
CONV_WIDTH = 4
ATTN_HEADS = 8
ATTN_WIDTH = MIX_WIDTH - MLSTM_WIDTH
ATTN_HEAD_DIM = ATTN_WIDTH // ATTN_HEADS
DILATED_PATTERNS = ((128, 1), (512, 4), (2048, 16))
ROPE_THETA = 500000.0
ROT_DIM = ATTN_HEAD_DIM // 4
PEER_KEYS = 128
PEER_EXPERTS = PEER_KEYS * PEER_KEYS
PEER_HEADS = 8
PEER_TOPK = 16
PEER_QUERY_DIM = 256
PEER_TOKEN_BLOCK = 128
DEEPNORM_ALPHA = (2.0 * DEPTH) ** 0.25
DEEPNORM_BETA = (8.0 * DEPTH) ** -0.25
LN_EPS = 1e-5

PROJ_SIZES = (MLSTM_WIDTH, MLSTM_WIDTH, MLSTM_WIDTH, MLSTM_WIDTH, MLSTM_HEADS, MLSTM_HEADS,
              ATTN_WIDTH, ATTN_WIDTH, ATTN_WIDTH)
IN_COLS = sum(PROJ_SIZES)

kernel_name = "hymba_mlstm_dilated_peer_layer"


def layer_norm(x, g, b):
    xf = x.astype(jnp.float32)
    mu = jnp.mean(xf, axis=-1, keepdims=True)
    var = jnp.mean(jnp.square(xf - mu), axis=-1, keepdims=True)
    y = (xf - mu) * lax.rsqrt(var + LN_EPS)
    return (y * g.astype(jnp.float32) + b.astype(jnp.float32)).astype(x.dtype)


def causal_depthwise_conv(x, w, b):
    c = x.shape[-1]
    kern = w[:, None, :].astype(x.dtype)
    y = lax.conv_general_dilated(x, kern, window_strides=(1,), padding=((CONV_WIDTH - 1, 0),),
                                 dimension_numbers=('NWC', 'WIO', 'NWC'), feature_group_count=c)
    return y + b.astype(x.dtype)


def mlstm_chunkwise(q, k, v, i_pre, log_f):
    bsz, nh, s, d = q.shape
    nc = s // MLSTM_CHUNK
    def chunks(t):
        t = t.reshape(bsz, nh, nc, MLSTM_CHUNK, *t.shape[3:])
        return jnp.moveaxis(t, 2, 0)
    causal = jnp.tril(jnp.ones((MLSTM_CHUNK, MLSTM_CHUNK), dtype=bool))

    def step(carry, inp):
        c_st, n_st, m_st = carry
        qc, kc, vc, ig, lf = inp
        bcum = jnp.cumsum(lf, axis=-1)
        dmat = bcum[..., :, None] - bcum[..., None, :] + ig[..., None, :]
        dmat = jnp.where(causal, dmat, -jnp.inf)
        m_inter = bcum + m_st[..., None]
        m_t = jnp.maximum(m_inter, jnp.max(dmat, axis=-1))
        wts = jnp.einsum('bhtd,bhsd->bhts', qc, kc) * jnp.exp(dmat - m_t[..., None])
        inter = jnp.exp(m_inter - m_t)
        num = jnp.einsum('bhts,bhsv->bhtv', wts, vc) + inter[..., None] * jnp.einsum('bhvd,bhtd->bhtv', c_st, qc)
        den = jnp.sum(wts, axis=-1) + inter * jnp.einsum('bhd,bhtd->bht', n_st, qc)
        h = num / jnp.maximum(jnp.abs(den), jnp.exp(-m_t))[..., None]
        b_last = bcum[..., -1]
        g = b_last[..., None] - bcum + ig
        m_new = jnp.maximum(b_last + m_st, jnp.max(g, axis=-1))
        decay = jnp.exp(b_last + m_st - m_new)
        wk = jnp.exp(g - m_new[..., None])
        c_new = decay[..., None, None] * c_st + jnp.einsum('bhs,bhsv,bhsd->bhvd', wk, vc, kc)
        n_new = decay[..., None] * n_st + jnp.einsum('bhs,bhsd->bhd', wk, kc)
        return (c_new, n_new, m_new), h

    init = (jnp.zeros((bsz, nh, d, d), jnp.float32), jnp.zeros((bsz, nh, d), jnp.float32),
            jnp.zeros((bsz, nh), jnp.float32))
    _, h = lax.scan(step, init, (chunks(q), chunks(k), chunks(v), chunks(i_pre), chunks(log_f)))
    return jnp.moveaxis(h, 0, 2).reshape(bsz, nh, s, d)


def rope_partial(x, pos):
    half = ROT_DIM // 2
    inv = ROPE_THETA ** (-jnp.arange(half, dtype=jnp.float32) / half)
    ang = pos[:, None] * inv[None, :]
    cos = jnp.cos(ang).astype(x.dtype)
    sin = jnp.sin(ang).astype(x.dtype)
    x1, x2, rest = x[..., :half], x[..., half:ROT_DIM], x[..., ROT_DIM:]
    return jnp.concatenate([x1 * cos - x2 * sin, x1 * sin + x2 * cos, rest], axis=-1)


def banded_causal_attention(q, k, v, window):
    bsz, g, l, dh = q.shape
    blk = window
    nb = -(-l // blk)
    pad = nb * blk - l
    def blocks(t):
        return jnp.pad(t, ((0, 0), (0, 0), (0, pad), (0, 0))).reshape(bsz, g, nb, blk, dh)
    qb, kb, vb = blocks(q), blocks(k), blocks(v)
    def with_prev(t):
        prev = jnp.pad(t[:, :, :-1], ((0, 0), (0, 0), (1, 0), (0, 0), (0, 0)))
        return jnp.concatenate([prev, t], axis=3)
    kk, vv = with_prev(kb), with_prev(vb)
    s = jnp.einsum('bgnqd,bgnkd->bgnqk', qb, kk).astype(jnp.float32) * (dh ** -0.5)
    bi = jnp.arange(nb)[:, None, None]
    qpos = bi * blk + jnp.arange(blk)[None, :, None]
    kpos = (bi - 1) * blk + jnp.arange(2 * blk)[None, None, :]
    dist = qpos - kpos
    valid = (dist >= 0) & (dist <= window) & (kpos >= 0)
    s = jnp.where(valid, s, -jnp.inf)
    m = jnp.max(s, axis=-1, keepdims=True)
    p = jnp.exp(s - m)
    den = jnp.sum(p, axis=-1)
    o = jnp.einsum('bgnqk,bgnkd->bgnqd', p, vv.astype(jnp.float32)) / den[..., None]
    lse = m[..., 0] + jnp.log(den)
    o = o.reshape(bsz, g, nb * blk, dh)[:, :, :l]
    lse = lse.reshape(bsz, g, nb * blk)[:, :, :l]
    return o, lse


def dilated_attention(q, k, v):
    bsz, nh, s, dh = q.shape
    outs, lses = [], []
    for window, dil in DILATED_PATTERNS:
        l = s // dil
        def to_sub(t):
            return t.reshape(bsz, nh, l, dil, dh).transpose(0, 1, 3, 2, 4).reshape(bsz, nh * dil, l, dh)
        o, lse = banded_causal_attention(to_sub(q), to_sub(k), to_sub(v), window // dil)
        outs.append(o.reshape(bsz, nh, dil, l, dh).transpose(0, 1, 3, 2, 4).reshape(bsz, nh, s, dh))
        lses.append(lse.reshape(bsz, nh, dil, l).transpose(0, 1, 3, 2).reshape(bsz, nh, s))
    wts = jax.nn.softmax(jnp.stack(lses, axis=0), axis=0)
    out = jnp.sum(wts[..., None] * jnp.stack(outs, axis=0), axis=0)
    return out.astype(q.dtype)


def token_mixer(x, w_in, conv_w, conv_b, b_igate, b_fgate, mh_norm_g, w_out):
    bsz, s, _ = x.shape
    proj = x @ w_in
    idx = np.cumsum(PROJ_SIZES)[:-1].tolist()
    mq, mk, mv, mo, mi, mf, aq, ak, av = jnp.split(proj, idx, axis=-1)
    qk = jax.nn.silu(causal_depthwise_conv(jnp.concatenate([mq, mk], axis=-1), conv_w, conv_b))
    mq, mk = qk[..., :MLSTM_WIDTH], qk[..., MLSTM_WIDTH:]
    def mheads(t):
        return t.reshape(bsz, s, MLSTM_HEADS, MLSTM_HEAD_DIM).transpose(0, 2, 1, 3).astype(jnp.float32)
    qh = mheads(mq)
    kh = mheads(mk) * (MLSTM_HEAD_DIM ** -0.5)
    vh = mheads(mv)
    i_pre = (mi + b_igate).astype(jnp.float32).transpose(0, 2, 1)
    log_f = jax.nn.log_sigmoid((mf + b_fgate).astype(jnp.float32)).transpose(0, 2, 1)
    h = mlstm_chunkwise(qh, kh, vh, i_pre, log_f)
    mu = jnp.mean(h, axis=-1, keepdims=True)
    var = jnp.mean(jnp.square(h - mu), axis=-1, keepdims=True)
    h = (h - mu) * lax.rsqrt(var + LN_EPS) * mh_norm_g.reshape(MLSTM_HEADS, 1, MLSTM_HEAD_DIM).astype(jnp.float32)
    h = h.transpose(0, 2, 1, 3).reshape(bsz, s, MLSTM_WIDTH).astype(x.dtype)
    out_a = jax.nn.sigmoid(mo) * h
    def aheads(t):
        return t.reshape(bsz, s, ATTN_HEADS, ATTN_HEAD_DIM).transpose(0, 2, 1, 3)
    pos = jnp.arange(s, dtype=jnp.float32)
    qa = rope_partial(aheads(aq), pos)
    ka = rope_partial(aheads(ak), pos)
    oa = dilated_attention(qa, ka, aheads(av))
    out_b = oa.transpose(0, 2, 1, 3).reshape(bsz, s, ATTN_WIDTH)
    return jnp.concatenate([out_a, out_b], axis=-1) @ w_out


def peer_ffn(x, w_query, sub_keys_1, sub_keys_2, expert_u, expert_v):
    bsz, s, d = x.shape
    t = bsz * s
    xt = x.reshape(t, d)
    q = (xt @ w_query).reshape(t, PEER_HEADS, PEER_QUERY_DIM)
    half = PEER_QUERY_DIM // 2
    s1 = jnp.einsum('thd,kd->thk', q[..., :half], sub_keys_1).astype(jnp.float32)
    s2 = jnp.einsum('thd,kd->thk', q[..., half:], sub_keys_2).astype(jnp.float32)
    v1, i1 = lax.top_k(s1, PEER_TOPK)
    v2, i2 = lax.top_k(s2, PEER_TOPK)
    cand = (v1[..., :, None] + v2[..., None, :]).reshape(t, PEER_HEADS, PEER_TOPK * PEER_TOPK)
    sc, ci = lax.top_k(cand, PEER_TOPK)
    e1 = jnp.take_along_axis(i1, ci // PEER_TOPK, axis=-1)
    e2 = jnp.take_along_axis(i2, ci % PEER_TOPK, axis=-1)
    experts = e1 * PEER_KEYS + e2
    gates = jax.nn.softmax(sc, axis=-1).astype(x.dtype)
    nb = t // PEER_TOKEN_BLOCK
    def block(args):
        xb, eb, gb = args
        hu = jnp.einsum('thkd,td->thk', expert_u[eb], xb)
        act = jax.nn.gelu(hu, approximate=False) * gb
        return jnp.einsum('thk,thkd->td', act, expert_v[eb])
    out = lax.map(block, (xt.reshape(nb, PEER_TOKEN_BLOCK, d),
                          experts.reshape(nb, PEER_TOKEN_BLOCK, PEER_HEADS, PEER_TOPK),
                          gates.reshape(nb, PEER_TOKEN_BLOCK, PEER_HEADS, PEER_TOPK)))
    return out.reshape(bsz, s, d)


def setup_inputs(seed: int = 0) -> dict:
    key = jax.random.key(seed)
    ks = jax.random.split(key, 17)
    d = D_MODEL
    nrm = jax.random.normal
    x = nrm(ks[0], (BATCH, SEQ, d), jnp.float32)
    col_scale = jnp.concatenate([
        jnp.ones((2 * MLSTM_WIDTH,), jnp.float32),
        jnp.full((MLSTM_WIDTH,), DEEPNORM_BETA, jnp.float32),
        jnp.ones((MLSTM_WIDTH + 2 * MLSTM_HEADS + 2 * ATTN_WIDTH,), jnp.float32),
        jnp.full((ATTN_WIDTH,), DEEPNORM_BETA, jnp.float32)])
    w_in = nrm(ks[1], (DEPTH, d, IN_COLS), jnp.float32) * (d ** -0.5) * col_scale
    conv_w = nrm(ks[2], (DEPTH, CONV_WIDTH, 2 * MLSTM_WIDTH), jnp.float32) * (CONV_WIDTH ** -0.5)
    conv_b = 0.01 * nrm(ks[3], (DEPTH, 2 * MLSTM_WIDTH), jnp.float32)
    b_igate = 0.1 * nrm(ks[4], (DEPTH, MLSTM_HEADS), jnp.float32)
    b_fgate = 3.0 + 0.5 * nrm(ks[5], (DEPTH, MLSTM_HEADS), jnp.float32)
    mh_norm_g = 1.0 + 0.02 * nrm(ks[6], (DEPTH, MLSTM_WIDTH), jnp.float32)
    w_out = nrm(ks[7], (DEPTH, MIX_WIDTH, d), jnp.float32) * (MIX_WIDTH ** -0.5) * DEEPNORM_BETA
    ln1_g = 1.0 + 0.02 * nrm(ks[8], (DEPTH, d), jnp.float32)
    ln1_b = 0.01 * nrm(ks[9], (DEPTH, d), jnp.float32)
    w_query = nrm(ks[10], (DEPTH, d, PEER_HEADS * PEER_QUERY_DIM), jnp.float32) * (d ** -0.5)
    sub_keys_1 = nrm(ks[11], (DEPTH, PEER_KEYS, PEER_QUERY_DIM // 2), jnp.float32) * ((PEER_QUERY_DIM // 2) ** -0.5)
    sub_keys_2 = nrm(ks[12], (DEPTH, PEER_KEYS, PEER_QUERY_DIM // 2), jnp.float32) * ((PEER_QUERY_DIM // 2) ** -0.5)
    expert_u = nrm(ks[13], (DEPTH, PEER_EXPERTS, d), jnp.float32) * (d ** -0.5)
    expert_v = nrm(ks[14], (DEPTH, PEER_EXPERTS, d), jnp.float32) * ((PEER_HEADS * PEER_TOPK) ** -0.5) * DEEPNORM_BETA
    ln2_g = 1.0 + 0.02 * nrm(ks[15], (DEPTH, d), jnp.float32)
    ln2_b = 0.01 * nrm(ks[16], (DEPTH, d), jnp.float32)
    return {"x": x, "w_in": w_in, "conv_w": conv_w, "conv_b": conv_b, "b_igate": b_igate,
            "b_fgate": b_fgate, "mh_norm_g": mh_norm_g, "w_out": w_out, "ln1_g": ln1_g, "ln1_b": ln1_b,
            "w_query": w_query, "sub_keys_1": sub_keys_1, "sub_keys_2": sub_keys_2,
            "expert_u": expert_u, "expert_v": expert_v, "ln2_g": ln2_g, "ln2_b": ln2_b}


def reference(x, w_in, conv_w, conv_b, b_igate, b_fgate, mh_norm_g, w_out, ln1_g, ln1_b,
              w_query, sub_keys_1, sub_keys_2, expert_u, expert_v, ln2_g, ln2_b):
    for l in range(DEPTH):
        mix = token_mixer(x, w_in[l], conv_w[l], conv_b[l], b_igate[l], b_fgate[l], mh_norm_g[l], w_out[l])
        x = layer_norm(DEEPNORM_ALPHA * x + mix, ln1_g[l], ln1_b[l])
        ffn = peer_ffn(x, w_query[l], sub_keys_1[l], sub_keys_2[l], expert_u[l], expert_v[l])
        x = layer_norm(DEEPNORM_ALPHA * x + ffn, ln2_g[l], ln2_b[l])
    return x
```

```python
import contextlib
import numpy as np
import concourse.bass as bass
import concourse.mybir as mybir
from concourse.bass_utils import run_bass_kernel_spmd

F32 = mybir.dt.float32
BF16 = mybir.dt.bfloat16
ALU = mybir.AluOpType
AF = mybir.ActivationFunctionType
AX = mybir.AxisListType

NSLOT = 8
D = 2048
NTOK = 2048
NALL = 4096
INC = 7176
ALPHA = 2.0 ** 0.25
LN_EPS = 1e-5
NEG = -30000.0


class Buf:
    def __init__(self, t, disjoint=False):
        self.t = t
        self.disjoint = disjoint
        self.writers = {}
        self.readers = {}

    def __getitem__(self, idx):
        return self.t[idx]


class Sched:
    ENGS = ("pe", "act", "dve", "pool", "sp")

    def __init__(self, nc, stack):
        self.nc = nc
        self.stack = stack
        self.ops = {e: [] for e in self.ENGS}
        self.sems = {}
        self.cnt = {}
        self.seen = {e: {} for e in self.ENGS}
        for e in ("pe", "act", "dve", "pool"):
            self.sems[e] = stack.enter_context(nc.semaphore("s_" + e))
            self.cnt[e] = 0
        for q in ("sp", "act", "pool"):
            for s in range(NSLOT):
                k = ("dma", q, s)
                self.sems[k] = stack.enter_context(nc.semaphore("d_%s%d" % (q, s)))
                self.cnt[k] = 0
        self.dma_rr = {"sp": 0, "act": 0, "pool": 0}
        self.nbuf = 0
        self.stacks = []
        self.marks = []

    def sb(self, shape, dt=F32, name=None):
        self.nbuf += 1
        t = self.stack.enter_context(self.nc.sbuf_tensor("sb_" + (name or ("%d" % self.nbuf)), list(shape), dt))
        return Buf(t)

    def ps(self, shape, dt=F32, name=None):
        self.nbuf += 1
        t = self.stack.enter_context(self.nc.psum_tensor("ps_" + (name or ("%d" % self.nbuf)), list(shape), dt))
        return Buf(t)

    def dram(self, name, shape, dt=F32):
        t = self.nc.dram_tensor(name, list(shape), dt, kind="Internal")
        return Buf(t, disjoint=True)

    def _need(self, eng, reads, writes):
        need = {}

        def add(d, skip_own=False):
            for c, v in d.items():
                if skip_own and c == eng:
                    continue
                if need.get(c, 0) < v:
                    need[c] = v

        for b in reads:
            add(b.writers)
        for b in writes:
            if not b.disjoint:
                add(b.writers)
            add(b.readers)
        out = []
        seen = self.seen[eng]
        for c, v in need.items():
            if seen.get(c, 0) < v:
                seen[c] = v
                out.append((c, v))
        return out

    def _commit(self, clock, val, reads, writes):
        for b in reads:
            if b.readers.get(clock, 0) < val:
                b.readers[clock] = val
        for b in writes:
            if b.disjoint:
                if b.writers.get(clock, 0) < val:
                    b.writers[clock] = val
            else:
                b.writers.clear()
                b.writers[clock] = val
                b.readers.clear()

    def op(self, eng, fn, reads=(), writes=()):
        waits = self._need(eng, reads, writes)
        self.cnt[eng] += 1
        val = self.cnt[eng]
        self.ops[eng].append((waits, fn, eng, 1))
        self._commit(eng, val, reads, writes)

    def dma(self, out_ap, in_ap, reads=(), writes=(), q="sp", **kw):
        s = self.dma_rr[q]
        self.dma_rr[q] = (s + 1) % NSLOT
        clock = ("dma", q, s)
        waits = self._need(q, reads, writes)
        prev = self.cnt[clock]
        if prev > 0 and self.seen[q].get(clock, 0) < prev:
            self.seen[q][clock] = prev
            waits.append((clock, prev))
        self.cnt[clock] += 16
        val = self.cnt[clock]

        def fn(e, out_ap=out_ap, in_ap=in_ap, kw=kw):
            return e.dma_start(out=out_ap, in_=in_ap, **kw)

        self.ops[q].append((waits, fn, clock, 16))
        self._commit(clock, val, reads, writes)

    def mark(self, name):
        self.marks.append((name, dict(self.cnt)))

    def push(self):
        st = contextlib.ExitStack()
        self.stacks.append(self.stack)
        self.stack = st

    def barrier(self):
        for e in self.ENGS:
            waits = []
            for c, v in self.cnt.items():
                if v > 0 and self.seen[e].get(c, 0) < v:
                    self.seen[e][c] = v
                    waits.append((c, v))
            self.ops[e].append((waits, None, None, 0))

    def pop(self, barrier=True):
        if barrier:
            self.barrier()
        self.stack.close()
        self.stack = self.stacks.pop()

    def pop_all(self):
        while self.stacks:
            self.pop(barrier=False)

    def finish(self, final_bufs):
        waits = self._need("sp", final_bufs, ())
        self.ops["sp"].append((waits, None, None, 0))

    def emit(self):
        nc = self.nc
        sems = self.sems

        def run(engname):
            def body(e):
                for waits, fn, clock, inc in self.ops[engname]:
                    for c, v in waits:
                        e.wait_ge(sems[c], v)
                    if fn is not None:
                        ins = fn(e)
                        ins.then_inc(sems[clock], inc)
            return body

        with nc.Block() as block:
            block.tensor(run("pe"))
            block.scalar(run("act"))
            block.vector(run("dve"))
            block.gpsimd(run("pool"))
            block.sync(run("sp"))


class RR:
    def __init__(self, items):
        self.items = list(items)
        self.i = 0

    def __call__(self):
        x = self.items[self.i % len(self.items)]
        self.i += 1
        return x


INPUT_SPECS = [
    ("xT", [D, NALL], F32), ("xown", [NTOK, D], F32),
    ("w_in", [D, INC], F32), ("conv_wT", [128, 16, 4], F32), ("conv_b", [128, 16], F32),
    ("b_ig", [4, 1], F32), ("b_fg", [4, 1], F32), ("mh_g", [1, 1024], F32),
    ("w_out", [D, D], F32), ("ln1_g", [1, D], F32), ("ln1_b", [1, D], F32),
    ("w_q", [D, D], F32), ("k1T", [128, 128], F32), ("k2T", [128, 128], F32),
    ("uT", [D, 16384], F32), ("ev", [16384, D], F32), ("ln2_g", [1, D], F32), ("ln2_b", [1, D], F32),
    ("flag", [128, 1], F32), ("cs", [NALL, 32], F32), ("maskA", [128, 256], F32), ("maskC", [128, 256], F32),
    ("ident", [128, 128], F32), ("cmask", [64, 64], F32),
]


def build_program(stop_after=99, dbg=()):
    nc = bass.Bass("TRN2", target_bir_lowering=False)
    I = {}
    for name, shape, dt in INPUT_SPECS:
        if stop_after < 5 and name in ("uT", "ev"):
            continue
        I[name] = nc.dram_tensor(name, shape, dt, kind="ExternalInput").ap()
    out_ap = nc.dram_tensor("out", [NTOK, D], F32, kind="ExternalOutput").ap()
    OUT = Buf(out_ap, disjoint=True)
    dbg_bufs = []

    with contextlib.ExitStack() as st:
        S = Sched(nc, st)
        P = lambda **k: None

        def dbg_out(name, src_buf, shape, dt):
            if name in dbg:
                o = nc.dram_tensor("dbg_" + name, list(shape), dt, kind="ExternalOutput").ap()
                ob = Buf(o, disjoint=True)
                S.dma(o, src_buf[:], reads=[src_buf], writes=[ob])
                dbg_bufs.append(ob)

        w_in_b = S.dram("w_in_b", [D, INC], BF16)
        w_out_b = S.dram("w_out_b", [D, D], BF16)
        w_q_b = S.dram("w_q_b", [D, D], BF16)
        uT_b = S.dram("uT_b", [D, 16384], BF16)
        ev_b = S.dram("ev_b", [16384, D], BF16)
        mqkT = S.dram("mqkT", [2048, NALL], F32)
        gT = S.dram("gT", [8, NALL], F32)
        mv_d = S.dram("mv_d", [NALL, 1024], F32)
        mo_d = S.dram("mo_d", [NALL, 1024], F32)
        aq_d = S.dram("aq_d", [NALL, 1024], F32)
        ak_d = S.dram("ak_d", [NALL, 1024], F32)
        av_d = S.dram("av_d", [NALL, 1024], BF16)

        def psum_std(nf=7, nb=1):
            F_ = [S.ps([128, 512], F32) for i in range(nf)]
            B_ = [S.ps([128, 1024], BF16) for i in range(nb)]
            return F_, B_

        identf = S.sb([128, 128], F32, name="identf")
        identb = S.sb([128, 128], BF16, name="identb")
        S.dma(identf[:, :], I["ident"], writes=[identf])
        S.op("dve", lambda e: e.tensor_copy(out=identb[:, :], in_=identf[:, :]), reads=[identf], writes=[identb])
        flag = S.sb([128, 1], F32, name="flag")
        S.dma(flag[:, :], I["flag"], writes=[flag])
        cmask = S.sb([64, 64], F32, name="cmask")
        S.dma(cmask[:, :], I["cmask"], writes=[cmask])
        scal = S.sb([64, 64, 12], F32, name="scal")
        decB = S.sb([128, 4, 64], F32, name="decB")

        S.push()
        PSF, _pb = psum_std()
        psbf = _pb[0]
        psf_rr = RR(PSF)
        ps_rr = psf_rr
        stg_f = [S.sb([128, 2048], F32, name="stgf%d" % i) for i in range(3)]
        stg_b = [S.sb([128, 2048], BF16, name="stgb%d" % i) for i in range(3)]
        cast_rr = RR(["dve", "act", "pool"])

        def cast(eng, out_ap, in_ap, reads, writes):
            if eng == "act":
                S.op("act", lambda e: e.activation(out=out_ap, in_=in_ap, func=AF.Copy), reads=reads, writes=writes)
            else:
                S.op(eng, lambda e: e.tensor_copy(out=out_ap, in_=in_ap), reads=reads, writes=writes)

        def convert(src_ap, dst, R, C):
            i = 0
            for r0 in range(0, R, 128):
                for c0 in range(0, C, 2048):
                    cc = min(2048, C - c0)
                    f = stg_f[i % 3]
                    b = stg_b[i % 3]
                    i += 1
                    S.dma(f[:, 0:cc], src_ap[r0:r0 + 128, c0:c0 + cc], writes=[f])
                    cast(cast_rr(), b[:, 0:cc], f[:, 0:cc], [f], [b])
                    S.dma(dst[r0:r0 + 128, c0:c0 + cc], b[:, 0:cc], reads=[b], writes=[dst], q="act")

        convert(I["w_in"], w_in_b, D, INC)
        convert(I["w_out"], w_out_b, D, D)
        convert(I["w_q"], w_q_b, D, D)
        if stop_after >= 5:
            convert(I["uT"], uT_b, D, 16384)
            convert(I["ev"], ev_b, 16384, D)

        S.mark("1")
        xf = [S.sb([128, 16, 512], F32, name="xf%d" % i) for i in range(1)]
        xb = [S.sb([128, 16, 512], BF16, name="xb%d" % i) for i in range(2)]
        wch = [S.sb([128, 16, 512], BF16, name="wch%d" % i) for i in range(2)]
        evo = [S.sb([128, 512], F32, name="evo%d" % i) for i in range(3)]
        evb = [S.sb([128, 512], BF16, name="evb%d" % i) for i in range(2)]
        xT3 = I["xT"].rearrange("(k p) t -> p k t", p=128)
        w3 = w_in_b[:, :].rearrange("(k p) c -> p k c", p=128)
        evac_rr = RR(["dve", "act"])
        it = 0
        for stile in range(8):
            t0 = stile * 512
            xbt = xb[stile % 2]
            for k in range(16):
                S.dma(xf[0][:, k, :], xT3[:, k, t0:t0 + 512], writes=[xf[0]])
            for k4 in range(4):
                cast(cast_rr(), xbt[:, k4 * 4:(k4 + 1) * 4, :], xf[0][:, k4 * 4:(k4 + 1) * 4, :], [xf[0]], [xbt])
            own = stile >= 4
            for cch in range(15):
                c0 = cch * 512
                cw = min(512, INC - c0)
                grp = c0 // 1024
                if c0 < 1024 and stile < 3:
                    continue
                if 3072 <= c0 < 4096 and not own:
                    continue
                wt = wch[it % 2]
                it += 1
                S.dma(wt[:, :, 0:cw], w3[:, :, c0:c0 + cw], reads=[w_in_b], writes=[wt])
                if c0 < 2048:
                    for sc in range(4):
                        ps = ps_rr()
                        def mm(e, ps=ps, wt=wt, sc=sc, xbt=xbt):
                            for k in range(16):
                                r = e.matmul(ps[:, :], lhsT=wt[:, k, sc * 128:(sc + 1) * 128], rhs=xbt[:, k, :],
                                             start=(k == 0), stop=(k == 15))
                            return r
                        S.op("pe", mm, reads=[wt, xbt], writes=[ps])
                        eo = evo[it % 3]
                        it += 1
                        cast(evac_rr(), eo[:, :], ps[:, :], [ps], [eo])
                        f0 = c0 + sc * 128
                        S.dma(mqkT[f0:f0 + 128, t0:t0 + 512], eo[:, :], reads=[eo], writes=[mqkT], q="act")
                else:
                    segs = []
                    for (lo, hi, dst, isbf) in ((2048, 3072, mv_d, False), (3072, 4096, mo_d, False),
                                                (4104, 5128, aq_d, False), (5128, 6152, ak_d, False),
                                                (6152, 7176, av_d, True)):
                        a = max(lo, c0)
                        b = min(hi, c0 + cw)
                        if a < b:
                            if dst is aq_d and not own:
                                continue
                            segs.append((a, b, dst, lo, isbf))
                    if c0 <= 4096 < c0 + cw:
                        ps = ps_rr()
                        g0 = 4096 - c0
                        def mmg(e, ps=ps, wt=wt, g0=g0, xbt=xbt):
                            for k in range(16):
                                r = e.matmul(ps[0:8, :], lhsT=wt[:, k, g0:g0 + 8], rhs=xbt[:, k, :],
                                             start=(k == 0), stop=(k == 15))
                            return r
                        S.op("pe", mmg, reads=[wt, xbt], writes=[ps])
                        eo = evo[it % 3]
                        it += 1
                        cast(evac_rr(), eo[0:8, :], ps[0:8, :], [ps], [eo])
                        S.dma(gT[0:8, t0:t0 + 512], eo[0:8, :], reads=[eo], writes=[gT], q="act")
                    for tt in range(4):
                        for (a, b, dst, lo, isbf) in segs:
                            ps = ps_rr()
                            n = b - a
                            def mmt(e, ps=ps, wt=wt, a=a, n=n, xbt=xbt, tt=tt, c0=c0):
                                for k in range(16):
                                    r = e.matmul(ps[:, 0:n], lhsT=xbt[:, k, tt * 128:(tt + 1) * 128],
                                                 rhs=wt[:, k, a - c0:a - c0 + n], start=(k == 0), stop=(k == 15))
                                return r
                            S.op("pe", mmt, reads=[wt, xbt], writes=[ps])
                            eo = (evb if isbf else evo)[it % 2]
                            it += 1
                            cast(evac_rr(), eo[:, 0:n], ps[:, 0:n], [ps], [eo])
                            r0 = t0 + tt * 128
                            S.dma(dst[r0:r0 + 128, a - lo:b - lo], eo[:, 0:n], reads=[eo], writes=[dst], q="act")

        for nm, bf, shp, dt in (("mqkT", mqkT, [2048, NALL], F32), ("gT", gT, [8, NALL], F32),
                                ("mv", mv_d, [NALL, 1024], F32), ("mo", mo_d, [NALL, 1024], F32),
                                ("aq", aq_d, [NALL, 1024], F32), ("ak", ak_d, [NALL, 1024], F32),
                                ("av", av_d, [NALL, 1024], BF16)):
            if nm in dbg:
                o = nc.dram_tensor("dbg_" + nm, shp, dt, kind="ExternalOutput").ap()
                ob = Buf(o, disjoint=True)
                S.dma(o, bf[:], reads=[bf], writes=[ob])
                dbg_bufs.append(ob)

        def dbg_dram(nm, bf, shp, dt):
            if nm in dbg:
                o = nc.dram_tensor("dbg_" + nm, list(shp), dt, kind="ExternalOutput").ap()
                ob = Buf(o, disjoint=True)
                S.dma(o, bf[:], reads=[bf], writes=[ob])
                dbg_bufs.append(ob)

        def early_exit():
            S.dma(out_ap[0:128, :], I["xown"][0:128, :], writes=[OUT])
            S.finish([OUT] + dbg_bufs)
            S.emit()
            S.pop_all()
            return nc

        if stop_after <= 1:
            return early_exit()

        S.pop()

        S.mark("2a")
        def v3(buf):
            return buf[:, :].rearrange("p (c t) -> p c t", t=64)

        S.push()
        PSF, _pb = psum_std()
        psbf = _pb[0]
        psf_rr = RR(PSF)
        ig = S.sb([4, NALL], F32, name="ig")
        fgt = S.sb([4, NALL], F32, name="fgt")
        S.dma(ig[:, :], gT[0:4, :], reads=[gT], writes=[ig])
        S.dma(fgt[:, :], gT[4:8, :], reads=[gT], writes=[fgt])
        big = S.sb([4, 1], F32, name="big")
        bfg = S.sb([4, 1], F32, name="bfg")
        S.dma(big[:, :], I["b_ig"], writes=[big])
        S.dma(bfg[:, :], I["b_fg"], writes=[bfg])
        nbf = S.sb([4, 1], F32, name="nbf")
        S.op("dve", lambda e: e.tensor_scalar(out=nbf[:, :], in0=bfg[:, :], scalar1=-1.0, scalar2=None, op0=ALU.mult),
             reads=[bfg], writes=[nbf])
        ga = S.sb([4, NALL], F32, name="ga")
        gb = fgt
        S.op("act", lambda e: e.activation(out=ga[:, :], in_=fgt[:, :], func=AF.Exp, bias=nbf[:, 0:1], scale=-1.0),
             reads=[fgt, nbf], writes=[ga])
        S.op("act", lambda e: e.activation(out=ga[:, :], in_=ga[:, :], func=AF.Ln, bias=1.0, scale=1.0),
             reads=[ga], writes=[ga])

        def logstep(src, tmp, op):
            cur, nxt = src, tmp
            for s_ in (1, 2, 4, 8, 16, 32):
                c3, n3 = v3(cur), v3(nxt)
                S.op("dve", lambda e, c3=c3, n3=n3, s_=s_: e.tensor_tensor(out=n3[:, :, s_:], in0=c3[:, :, s_:], in1=c3[:, :, :64 - s_], op=op),
                     reads=[cur], writes=[nxt])
                S.op("pool", lambda e, c3=c3, n3=n3, s_=s_: e.tensor_copy(out=n3[:, :, :s_], in_=c3[:, :, :s_]),
                     reads=[cur], writes=[nxt])
                cur, nxt = nxt, cur
            return cur

        cs_ = logstep(ga, gb, ALU.add)
        gg = S.sb([4, NALL], F32, name="gg")
        gg2 = ig
        S.op("dve", lambda e: e.scalar_tensor_tensor(out=gg[:, :], in0=ig[:, :], scalar=big[:, 0:1], in1=cs_[:, :], op0=ALU.add, op1=ALU.add),
             reads=[ig, big, cs_], writes=[gg])
        gsave = S.sb([4, NALL], F32, name="gsave")
        S.op("pool", lambda e: e.tensor_copy(out=gsave[:, :], in_=gg[:, :]), reads=[gg], writes=[gsave])
        cm = logstep(gg, gg2, ALU.max)
        mst = S.sb([4, 65], F32, name="mst")
        Mc = S.sb([4, 64], F32, name="Mc")
        S.op("dve", lambda e: e.memset(mst[:, :], 0.0), writes=[mst])
        cm3 = v3(cm)
        cs3 = v3(cs_)
        for c in range(64):
            if c == 32:
                S.op("dve", lambda e: e.tensor_tensor(out=mst[:, 32:33], in0=mst[:, 32:33], in1=flag[0:4, 0:1], op=ALU.mult),
                     reads=[mst, flag], writes=[mst])
            S.op("dve", lambda e, c=c: e.tensor_tensor(out=Mc[:, c:c + 1], in0=mst[:, c:c + 1], in1=cm3[:, c, 63:64], op=ALU.max),
                 reads=[mst, cm], writes=[Mc])
            S.op("dve", lambda e, c=c: e.tensor_tensor(out=mst[:, c + 1:c + 2], in0=Mc[:, c:c + 1], in1=cs3[:, c, 63:64], op=ALU.subtract),
                 reads=[Mc, cs_], writes=[mst])
        MT = S.sb([4, NALL], F32, name="MT")
        bc_m = lambda b_: b_[:, 0:64].unsqueeze(2).to_broadcast([4, 64, 64])
        S.op("dve", lambda e: e.tensor_tensor(out=v3(MT), in0=cm3, in1=bc_m(mst), op=ALU.max), reads=[cm, mst], writes=[MT])
        stk = S.sb([12, NALL], F32, name="stk")
        tq = [S.sb([4, NALL], F32, name="tq%d" % i) for i in range(3)]
        S.op("dve", lambda e: e.tensor_tensor(out=v3(tq[0]), in0=v3(gsave), in1=bc_m(Mc), op=ALU.subtract), reads=[gsave, Mc], writes=[tq[0]])
        S.op("act", lambda e: e.activation(out=tq[0][:, :], in_=tq[0][:, :], func=AF.Exp), reads=[tq[0]], writes=[tq[0]])
        S.op("dve", lambda e: e.tensor_tensor(out=v3(tq[1]), in0=bc_m(Mc), in1=v3(MT), op=ALU.subtract), reads=[MT, Mc], writes=[tq[1]])
        S.op("act", lambda e: e.activation(out=tq[1][:, :], in_=tq[1][:, :], func=AF.Exp), reads=[tq[1]], writes=[tq[1]])
        S.op("dve", lambda e: e.tensor_tensor(out=tq[2][:, :], in0=cs_[:, :], in1=MT[:, :], op=ALU.subtract), reads=[MT, cs_], writes=[tq[2]])
        S.op("act", lambda e: e.activation(out=tq[2][:, :], in_=tq[2][:, :], func=AF.Exp), reads=[tq[2]], writes=[tq[2]])
        for i in range(3):
            S.dma(stk[4 * i:4 * i + 4, :], tq[i][:, :], reads=[tq[i]], writes=[stk])
        dec = S.sb([4, 64], F32, name="dec")
        S.op("dve", lambda e: e.tensor_tensor(out=dec[:, :], in0=mst[:, 0:64], in1=Mc[:, :], op=ALU.subtract), reads=[mst, Mc], writes=[dec])
        S.op("act", lambda e: e.activation(out=dec[:, :], in_=dec[:, :], func=AF.Exp), reads=[dec], writes=[dec])
        S.op("dve", lambda e: e.tensor_tensor(out=dec[:, 32:33], in0=dec[:, 32:33], in1=flag[0:4, 0:1], op=ALU.mult), reads=[dec, flag], writes=[dec])
        for half_ in range(2):
            ps = psf_rr()
            def tr(e, ps=ps, half_=half_):
                for cc_ in range(32):
                    c = half_ * 32 + cc_
                    r = e.transpose(out=ps[0:64, cc_ * 12:(cc_ + 1) * 12], in_=stk[0:12, c * 64:(c + 1) * 64], identity=identf[0:12, 0:12])
                return r
            S.op("pe", tr, reads=[stk, identf], writes=[ps])
            S.op("dve", lambda e, ps=ps, half_=half_: e.tensor_copy(out=scal[:, half_ * 32:(half_ + 1) * 32, :].rearrange("p c k -> p (c k)"), in_=ps[0:64, 0:384]),
                 reads=[ps], writes=[scal])
        sel = S.sb([4, 4, 128], F32, name="sel")
        S.op("dve", lambda e: e.tensor_copy(out=sel[:, :, :], in_=identf[0:4, 0:4].unsqueeze(2).to_broadcast([4, 4, 128])), reads=[identf], writes=[sel])
        ps = psf_rr()
        def bcm(e, ps=ps):
            for h in range(4):
                r = e.matmul(ps[:, h * 64:(h + 1) * 64], lhsT=sel[0:4, h, :], rhs=dec[0:4, :], start=True, stop=True)
            return r
        S.op("pe", bcm, reads=[sel, dec], writes=[ps])
        S.op("dve", lambda e, ps=ps: e.tensor_copy(out=decB[:, :, :].rearrange("p h c -> p (h c)"), in_=ps[:, 0:256]), reads=[ps], writes=[decB])
        if "scal" in dbg:
            dbg_out("scal", scal, [64, 64, 12], F32)
            dbg_out("decB", decB, [128, 4, 64], F32)

        S.pop()
        S.push()
        PSF, _pb = psum_std()
        psbf = _pb[0]
        psf_rr = RR(PSF)
        S.mark("2b")
        qTs = S.sb([128, 8, NTOK], BF16, name="qTs")
        kTs = S.sb([128, 8, NALL], BF16, name="kTs")
        ktok_d = S.dram("ktok_d", [NALL, 1024], BF16)
        cw = S.sb([128, 16, 4], F32, name="cw")
        cb = S.sb([128, 16], F32, name="cb")
        S.dma(cw[:, :, :], I["conv_wT"], writes=[cw])
        S.dma(cb[:, :], I["conv_b"], writes=[cb])
        raw = [S.sb([128, 2051], F32, name="raw%d" % i) for i in range(2)]
        acc = [S.sb([128, 2048], F32, name="acc%d" % i) for i in range(2)]
        it = 0
        for fc in range(16):
            isk = fc >= 8
            for tb in ([0, 2048] if isk else [2048]):
                rw = raw[it % 2]
                ac = acc[it % 2]
                it += 1
                S.dma(rw[:, 3:2051], mqkT[fc * 128:(fc + 1) * 128, tb:tb + 2048], reads=[mqkT], writes=[rw])
                if tb == 0:
                    S.op("pool", lambda e, rw=rw: e.memset(rw[:, 0:3], 0.0), writes=[rw])
                else:
                    S.dma(rw[:, 0:3], mqkT[fc * 128:(fc + 1) * 128, tb - 3:tb], reads=[mqkT], writes=[rw])
                S.op("dve", lambda e, rw=rw, ac=ac, fc=fc: e.tensor_scalar(out=ac[:, :], in0=rw[:, 0:2048], scalar1=cw[:, fc, 0:1], scalar2=None, op0=ALU.mult),
                     reads=[rw, cw], writes=[ac])
                for j in (1, 2, 3):
                    eng = "dve"
                    S.op(eng, lambda e, rw=rw, ac=ac, fc=fc, j=j: e.scalar_tensor_tensor(out=ac[:, :], in0=rw[:, j:j + 2048], scalar=cw[:, fc, j:j + 1], in1=ac[:, :], op0=ALU.mult, op1=ALU.add),
                         reads=[rw, cw, ac], writes=[ac])
                if isk:
                    dst = kTs[:, fc - 8, tb:tb + 2048]
                    dbuf = kTs
                else:
                    dst = qTs[:, fc, :]
                    dbuf = qTs
                S.op("act", lambda e, ac=ac, dst=dst, fc=fc: e.activation(out=dst, in_=ac[:, :], func=AF.Silu, bias=cb[:, fc:fc + 1]),
                     reads=[ac, cb], writes=[dbuf])
                if isk:
                    S.op("pool", lambda e, dst=dst: e.tensor_scalar(out=dst, in0=dst, scalar1=0.0625, scalar2=None, op0=ALU.mult),
                         reads=[kTs], writes=[kTs])
        ktb = [S.sb([128, 1024], BF16, name="ktb%d" % i) for i in range(2)]
        for tb in range(32):
            def trk(e, tb=tb, psbf=psbf):
                for f in range(8):
                    r = e.transpose(out=psbf[:, f * 128:(f + 1) * 128], in_=kTs[:, f, tb * 128:(tb + 1) * 128], identity=identb[:, :])
                return r
            S.op("pe", trk, reads=[kTs, identb], writes=[psbf])
            kb = ktb[tb % 2]
            cast(evac_rr(), kb[:, :], psbf[:, :], [psbf], [kb])
            S.dma(ktok_d[tb * 128:(tb + 1) * 128, :], kb[:, :], reads=[kb], writes=[ktok_d], q="act")
        if "qTs" in dbg:
            dbg_out("qTs", qTs, [128, 8, NTOK], BF16)
            dbg_out("kTs", kTs, [128, 8, NALL], BF16)
            dbg_dram("ktok", ktok_d, [NALL, 1024], BF16)

        S.mark("2c")
        cat_d = S.dram("cat_d", [NTOK, 2048], BF16)
        gB = S.sb([64, 1024], F32, name="gB")
        S.dma(gB[:, :], I["mh_g"].to_broadcast([64, 1024]), writes=[gB])
        STf = [S.sb([128, 2, 257], F32, name="STf%d" % h) for h in range(4)]
        STb = [S.sb([128, 2, 257], BF16, name="STb%d" % h) for h in range(4)]
        for h in range(4):
            S.op("pool", lambda e, h=h: e.memset(STf[h][:, :, :], 0.0), writes=[STf[h]])
        kcs = [S.sb([64, 1024], BF16, name="kc%d" % i) for i in range(3)]
        vcs = [S.sb([64, 1024], F32, name="vc%d" % i) for i in range(3)]
        mos = [S.sb([64, 1024], F32, name="moc%d" % i) for i in range(2)]
        sgs = [S.sb([64, 1024], F32, name="sg%d" % i) for i in range(2)]
        vps = [S.sb([64, 257], BF16, name="vp%d" % i) for i in range(8)]
        Sms = [S.sb([64, 64], BF16, name="Sm%d" % i) for i in range(4)]
        sm1 = [S.sb([64, 8], F32, name="sm1_%d" % i) for i in range(4)]
        hns = [S.sb([64, 256], F32, name="hn%d" % i) for i in range(4)]
        jnk = [S.sb([64, 256], F32, name="jnk%d" % i) for i in range(2)]
        catA = [S.sb([64, 1024], BF16, name="catA%d" % i) for i in range(2)]
        flat = lambda b_: b_[:, :, :].rearrange("p a b -> p (a b)")
        for c in range(64):
            own = c >= 32
            kc = kcs[c % 3]
            vc = vcs[c % 3]
            S.dma(kc[:, :], ktok_d[c * 64:(c + 1) * 64, :], reads=[ktok_d], writes=[kc])
            S.dma(vc[:, :], mv_d[c * 64:(c + 1) * 64, :], reads=[mv_d], writes=[vc])
            if own:
                mo_ = mos[c % 2]
                sg = sgs[c % 2]
                ca = catA[c % 2]
                S.dma(mo_[:, :], mo_d[c * 64:(c + 1) * 64, :], reads=[mo_d], writes=[mo_])
                S.op("act", lambda e, mo_=mo_, sg=sg: e.activation(out=sg[:, :], in_=mo_[:, :], func=AF.Sigmoid), reads=[mo_], writes=[sg])
            for h in range(4):
                vp = vps[(c * 4 + h) % 8]
                p_ap = scal[:, c, h:h + 1]
                r_ap = scal[:, c, 4 + h:5 + h]
                f_ap = scal[:, c, 8 + h:9 + h]
                S.op("dve", lambda e, vp=vp, vc=vc, h=h, p_ap=p_ap: e.tensor_scalar(out=vp[:, 0:256], in0=vc[:, h * 256:(h + 1) * 256], scalar1=p_ap, scalar2=None, op0=ALU.mult),
                     reads=[vc, scal], writes=[vp])
                S.op("pool", lambda e, vp=vp, p_ap=p_ap: e.tensor_copy(out=vp[:, 256:257], in_=p_ap), reads=[scal, vp], writes=[vp])
                S.op("dve", lambda e, h=h, c=c: e.tensor_scalar(out=flat(STf[h]), in0=flat(STf[h]), scalar1=decB[:, h, c:c + 1], scalar2=None, op0=ALU.mult),
                     reads=[STf[h], decB], writes=[STf[h]])
                if own:
                    S.op("act", lambda e, h=h: e.activation(out=flat(STb[h]), in_=flat(STf[h]), func=AF.Copy), reads=[STf[h]], writes=[STb[h]])
                    co = c - 32
                    psS = psf_rr()
                    def mmS(e, psS=psS, h=h, c=c, co=co):
                        for dc in range(2):
                            r = e.matmul(psS[0:64, 0:64], lhsT=kTs[:, h * 2 + dc, c * 64:(c + 1) * 64], rhs=qTs[:, h * 2 + dc, co * 64:(co + 1) * 64],
                                         start=(dc == 0), stop=(dc == 1))
                        return r
                    S.op("pe", mmS, reads=[kTs, qTs], writes=[psS])
                    Sm = Sms[h]
                    S.op("dve", lambda e, psS=psS, Sm=Sm: e.tensor_tensor(out=Sm[:, :], in0=psS[0:64, 0:64], in1=cmask[:, :], op=ALU.mult),
                         reads=[psS, cmask], writes=[Sm])
                    psO = psf_rr()
                    def mmO(e, psO=psO, Sm=Sm, vp=vp, h=h, co=co):
                        e.matmul(psO[0:64, 0:257], lhsT=Sm[:, :], rhs=vp[:, :], start=True, stop=False)
                        for dc in range(2):
                            r = e.matmul(psO[0:64, 0:257], lhsT=qTs[:, h * 2 + dc, co * 64:(co + 1) * 64], rhs=STb[h][:, dc, :],
                                         start=False, stop=(dc == 1))
                        return r
                    S.op("pe", mmO, reads=[Sm, vp, qTs, STb[h]], writes=[psO])
                psU = [psf_rr(), psf_rr()]
                for dc in range(2):
                    S.op("pe", lambda e, dc=dc, psU=psU, kc=kc, vp=vp, h=h: e.matmul(psU[dc][:, 0:257], lhsT=kc[:, h * 256 + dc * 128:h * 256 + (dc + 1) * 128], rhs=vp[:, :], start=True, stop=True),
                         reads=[kc, vp], writes=[psU[dc]])
                for dc in range(2):
                    eng = "dve" if dc == 0 else "pool"
                    if eng == "pool":
                        eng = "dve"
                    S.op(eng, lambda e, dc=dc, psU=psU, h=h: e.tensor_tensor(out=STf[h][:, dc, :], in0=STf[h][:, dc, :], in1=psU[dc][:, 0:257], op=ALU.add),
                         reads=[STf[h], psU[dc]], writes=[STf[h]])
                if own:
                    s1 = sm1[h]
                    hn = hns[h]
                    jk = jnk[h % 2]
                    S.op("dve", lambda e, s1=s1, psO=psO, r_ap=r_ap: e.tensor_tensor(out=s1[:, 0:1], in0=psO[0:64, 256:257], in1=r_ap, op=ALU.mult),
                         reads=[psO, scal], writes=[s1])
                    S.op("dve", lambda e, s1=s1: e.tensor_scalar(out=s1[:, 7:8], in0=s1[:, 0:1], scalar1=-1.0, scalar2=None, op0=ALU.mult),
                         reads=[s1], writes=[s1])
                    S.op("dve", lambda e, s1=s1: e.tensor_tensor(out=s1[:, 7:8], in0=s1[:, 7:8], in1=s1[:, 0:1], op=ALU.max),
                         reads=[s1], writes=[s1])
                    S.op("dve", lambda e, s1=s1, f_ap=f_ap: e.tensor_tensor(out=s1[:, 1:2], in0=s1[:, 7:8], in1=f_ap, op=ALU.max),
                         reads=[s1, scal], writes=[s1])
                    S.op("dve", lambda e, s1=s1: e.reciprocal(out=s1[:, 2:3], in_=s1[:, 1:2]), reads=[s1], writes=[s1])
                    S.op("dve", lambda e, s1=s1, r_ap=r_ap: e.tensor_tensor(out=s1[:, 3:4], in0=s1[:, 2:3], in1=r_ap, op=ALU.mult),
                         reads=[s1, scal], writes=[s1])
                    S.op("act", lambda e, hn=hn, psO=psO, s1=s1: e.activation(out=hn[:, :], in_=psO[0:64, 0:256], func=AF.Copy, scale=s1[:, 3:4]),
                         reads=[psO, s1], writes=[hn])
                    S.op("dve", lambda e, hn=hn, s1=s1: e.reduce_sum(out=s1[:, 4:5], in_=hn[:, :], axis=AX.X), reads=[hn, s1], writes=[s1])
                    S.op("dve", lambda e, s1=s1: e.tensor_scalar(out=s1[:, 4:5], in0=s1[:, 4:5], scalar1=-1.0 / 256.0, scalar2=None, op0=ALU.mult),
                         reads=[s1], writes=[s1])
                    S.op("act", lambda e, hn=hn, jk=jk, s1=s1: e.activation(out=jk[:, :], in_=hn[:, :], func=AF.Square, bias=s1[:, 4:5]),
                         reads=[hn, s1], writes=[jk])
                    S.op("dve", lambda e, jk=jk, s1=s1: e.reduce_sum(out=s1[:, 5:6], in_=jk[:, :], axis=AX.X), reads=[jk, s1], writes=[s1])
                    S.op("dve", lambda e, s1=s1: e.tensor_scalar(out=s1[:, 5:6], in0=s1[:, 5:6], scalar1=1.0 / 256.0, scalar2=LN_EPS, op0=ALU.mult, op1=ALU.add),
                         reads=[s1], writes=[s1])
                    S.op("act", lambda e, s1=s1: e.activation(out=s1[:, 6:7], in_=s1[:, 5:6], func=AF.Ln), reads=[s1], writes=[s1])
                    S.op("act", lambda e, s1=s1: e.activation(out=s1[:, 5:6], in_=s1[:, 6:7], func=AF.Exp, scale=-0.5), reads=[s1], writes=[s1])
                    S.op("dve", lambda e, hn=hn, s1=s1: e.tensor_scalar(out=hn[:, :], in0=hn[:, :], scalar1=s1[:, 4:5], scalar2=s1[:, 5:6], op0=ALU.add, op1=ALU.mult),
                         reads=[hn, s1], writes=[hn])
                    S.op("pool", lambda e, hn=hn, h=h: e.tensor_tensor(out=hn[:, :], in0=hn[:, :], in1=gB[:, h * 256:(h + 1) * 256], op=ALU.mult),
                         reads=[hn, gB], writes=[hn])
                    S.op("pool", lambda e, hn=hn, h=h, sg=sg, ca=ca: e.tensor_tensor(out=ca[:, h * 256:(h + 1) * 256], in0=hn[:, :], in1=sg[:, h * 256:(h + 1) * 256], op=ALU.mult),
                         reads=[hn, sg], writes=[ca])
            if own:
                S.dma(cat_d[(c - 32) * 64:(c - 31) * 64, 0:1024], ca[:, :], reads=[ca], writes=[cat_d], q="act")
        dbg_dram("cat", cat_d, [NTOK, 2048], BF16)
        if stop_after <= 2:
            return early_exit()
        S.pop()

        S.mark("3")
        S.push()
        PSF, _pb = psum_std()
        psbf = _pb[0]
        psf_rr = RR(PSF)
        SCALE = 128.0 ** -0.5
        anum_d = [S.dram("anum%d" % p, [NTOK, 1024], F32) for p in range(3)]
        am_d = [S.dram("am%d" % p, [NTOK, 8], F32) for p in range(3)]
        ad_d = [S.dram("ad%d" % p, [NTOK, 8], F32) for p in range(3)]
        qTa = S.sb([128, 8, NTOK], BF16, name="qTa")
        kTa = S.sb([128, 8, NALL], BF16, name="kTa")
        maskA = S.sb([128, 256], F32, name="maskA")
        maskC = S.sb([128, 256], F32, name="maskC")
        S.dma(maskA[:, :], I["maskA"], writes=[maskA])
        S.dma(maskC[:, :], I["maskC"], writes=[maskC])
        xts = [S.sb([128, 1024], F32, name="rx%d" % i) for i in range(2)]
        xbs = [S.sb([128, 1024], BF16, name="rxb%d" % i) for i in range(2)]
        csts = [S.sb([128, 32], F32, name="cst%d" % i) for i in range(2)]
        rt = [S.sb([128, 8, 16], F32, name="rt%d" % i) for i in range(4)]
        it = 0
        for (src, dstT, t_lo) in ((ak_d, kTa, 0), (aq_d, qTa, 16)):
            for tt in range(t_lo, 32):
                xt = xts[it % 2]
                xbt = xbs[it % 2]
                cst = csts[it % 2]
                it += 1
                S.dma(xt[:, :], src[tt * 128:(tt + 1) * 128, :], reads=[src], writes=[xt])
                S.dma(cst[:, :], I["cs"][tt * 128:(tt + 1) * 128, :], writes=[cst])
                cast("act", xbt[:, :], xt[:, :], [xt], [xbt])
                x3 = xt[:, :].rearrange("p (h d) -> p h d", d=128)
                xb3 = xbt[:, :].rearrange("p (h d) -> p h d", d=128)
                cosb = cst[:, 0:16].unsqueeze(1).to_broadcast([128, 8, 16])
                sinb = cst[:, 16:32].unsqueeze(1).to_broadcast([128, 8, 16])
                S.op("dve", lambda e, x3=x3, cosb=cosb: e.tensor_tensor(out=rt[0][:, :, :], in0=x3[:, :, 0:16], in1=cosb, op=ALU.mult), reads=[xt, cst], writes=[rt[0]])
                S.op("pool", lambda e, x3=x3, sinb=sinb: e.tensor_tensor(out=rt[1][:, :, :], in0=x3[:, :, 16:32], in1=sinb, op=ALU.mult), reads=[xt, cst], writes=[rt[1]])
                S.op("dve", lambda e, x3=x3, sinb=sinb: e.tensor_tensor(out=rt[2][:, :, :], in0=x3[:, :, 0:16], in1=sinb, op=ALU.mult), reads=[xt, cst], writes=[rt[2]])
                S.op("pool", lambda e, x3=x3, cosb=cosb: e.tensor_tensor(out=rt[3][:, :, :], in0=x3[:, :, 16:32], in1=cosb, op=ALU.mult), reads=[xt, cst], writes=[rt[3]])
                S.op("dve", lambda e, xb3=xb3: e.tensor_tensor(out=xb3[:, :, 0:16], in0=rt[0][:, :, :], in1=rt[1][:, :, :], op=ALU.subtract), reads=[rt[0], rt[1], xbt], writes=[xbt])
                S.op("dve", lambda e, xb3=xb3: e.tensor_tensor(out=xb3[:, :, 16:32], in0=rt[2][:, :, :], in1=rt[3][:, :, :], op=ALU.add), reads=[rt[2], rt[3], xbt], writes=[xbt])
                def trr(e, xbt=xbt, psbf=psbf):
                    for h in range(8):
                        r = e.transpose(out=psbf[:, h * 128:(h + 1) * 128], in_=xbt[:, h * 128:(h + 1) * 128], identity=identb[:, :])
                    return r
                S.op("pe", trr, reads=[xbt, identb], writes=[psbf])
                tl = tt - t_lo
                cast(evac_rr(), dstT[:, :, tl * 128:(tl + 1) * 128], psbf[:, :].rearrange("p (h t) -> p h t", t=128), [psbf], [dstT])
        vts = [S.sb([128, 2, 1024], BF16, name="vt%d" % i) for i in range(2)]
        sms = [S.sb([128, 256], F32, name="sm%d" % i) for i in range(2)]
        Pbs = [S.sb([128, 256], BF16, name="Pb%d" % i) for i in range(2)]
        PTs = [S.sb([128, 2, 128], BF16, name="PT%d" % i) for i in range(2)]
        mxs = [S.sb([128, 2], F32, name="mx%d" % i) for i in range(2)]
        resN = [S.sb([128, 8, 128], F32, name="resN%d" % i) for i in range(2)]
        resM = [S.sb([128, 8], F32, name="resM%d" % i) for i in range(2)]
        resD = [S.sb([128, 8], F32, name="resD%d" % i) for i in range(2)]
        u = 0
        for pi, (W_, Dl) in enumerate(((128, 1), (512, 4), (2048, 16))):
            nblk = 16 // Dl
            av_v = av_d[:, :].rearrange("(l d) c -> d l c", d=Dl)
            for r_ in range(Dl):
                for n_ in range(nblk):
                    nbg = nblk + n_
                    vt = vts[u % 2]
                    rN, rM, rD = resN[u % 2], resM[u % 2], resD[u % 2]
                    u += 1
                    S.dma(vt[:, :, :], av_v[r_, (nbg - 1) * 128:(nbg + 1) * 128, :].rearrange("(b p) c -> p b c", p=128), reads=[av_d], writes=[vt])
                    mk_ = maskC if n_ == 0 else maskA
                    for h in range(8):
                        q_ap = qTa[:, h, :].rearrange("p (l d) -> p d l", d=Dl)[:, r_, n_ * 128:(n_ + 1) * 128]
                        k_ap = kTa[:, h, :].rearrange("p (l d) -> p d l", d=Dl)[:, r_, (nbg - 1) * 128:(nbg + 1) * 128]
                        psS = psf_rr()
                        S.op("pe", lambda e, psS=psS, q_ap=q_ap, k_ap=k_ap: e.matmul(psS[:, 0:256], lhsT=q_ap, rhs=k_ap, start=True, stop=True), reads=[qTa, kTa], writes=[psS])
                        sm = sms[h % 2]
                        Pb = Pbs[h % 2]
                        PT = PTs[h % 2]
                        mx = mxs[h % 2]
                        S.op("dve", lambda e, sm=sm, psS=psS, mk_=mk_: e.scalar_tensor_tensor(out=sm[:, :], in0=psS[:, 0:256], scalar=SCALE, in1=mk_[:, :], op0=ALU.mult, op1=ALU.add),
                             reads=[psS, mk_], writes=[sm])
                        S.op("dve", lambda e, sm=sm, rM=rM, h=h: e.reduce_max(out=rM[:, h:h + 1], in_=sm[:, :], axis=AX.X), reads=[sm, rM], writes=[rM])
                        S.op("dve", lambda e, mx=mx, rM=rM, h=h: e.tensor_scalar(out=mx[:, 0:1], in0=rM[:, h:h + 1], scalar1=-1.0, scalar2=None, op0=ALU.mult), reads=[rM], writes=[mx])
                        S.op("act", lambda e, Pb=Pb, sm=sm, mx=mx: e.activation(out=Pb[:, :], in_=sm[:, :], func=AF.Exp, bias=mx[:, 0:1]), reads=[sm, mx], writes=[Pb])
                        S.op("dve", lambda e, Pb=Pb, rD=rD, h=h: e.reduce_sum(out=rD[:, h:h + 1], in_=Pb[:, :], axis=AX.X), reads=[Pb, rD], writes=[rD])
                        def trp(e, Pb=Pb, psbf=psbf):
                            e.transpose(out=psbf[:, 0:128], in_=Pb[:, 0:128], identity=identb[:, :])
                            return e.transpose(out=psbf[:, 128:256], in_=Pb[:, 128:256], identity=identb[:, :])
                        S.op("pe", trp, reads=[Pb, identb], writes=[psbf])
                        cast("act", PT[:, :, :].rearrange("p b q -> p (b q)"), psbf[:, 0:256], [psbf], [PT])
                        psO = psf_rr()
                        def mmo(e, psO=psO, PT=PT, vt=vt, h=h):
                            e.matmul(psO[:, 0:128], lhsT=PT[:, 0, :], rhs=vt[:, 0, h * 128:(h + 1) * 128], start=True, stop=False)
                            return e.matmul(psO[:, 0:128], lhsT=PT[:, 1, :], rhs=vt[:, 1, h * 128:(h + 1) * 128], start=False, stop=True)
                        S.op("pe", mmo, reads=[PT, vt], writes=[psO])
                        S.op("pool" if False else "dve", lambda e, rN=rN, psO=psO, h=h: e.tensor_copy(out=rN[:, h, :], in_=psO[:, 0:128]), reads=[psO, rN], writes=[rN])
                    rows = lambda dd: dd[:, :].rearrange("(l d) c -> d l c", d=Dl)[r_, n_ * 128:(n_ + 1) * 128, :]
                    S.dma(rows(anum_d[pi]), rN[:, :, :].rearrange("p h d -> p (h d)"), reads=[rN], writes=[anum_d[pi]], q="act")
                    S.dma(rows(am_d[pi]), rM[:, :], reads=[rM], writes=[am_d[pi]], q="act")
                    S.dma(rows(ad_d[pi]), rD[:, :], reads=[rD], writes=[ad_d[pi]], q="act")
        S.mark("3merge")
        n3s = [S.sb([128, 3, 1024], F32, name="n3_%d" % i) for i in range(2)]
        m3s = [S.sb([128, 3, 8], F32, name="m3_%d" % i) for i in range(2)]
        d3s = [S.sb([128, 3, 8], F32, name="d3_%d" % i) for i in range(2)]
        w3s = [S.sb([128, 3, 8], F32, name="w3_%d" % i) for i in range(2)]
        mMs = [S.sb([128, 16], F32, name="mM_%d" % i) for i in range(2)]
        obs = [S.sb([128, 1024], BF16, name="ob_%d" % i) for i in range(2)]
        for tt in range(16):
            n3, m3, d3, w3_, mM, ob = n3s[tt % 2], m3s[tt % 2], d3s[tt % 2], w3s[tt % 2], mMs[tt % 2], obs[tt % 2]
            for p in range(3):
                S.dma(n3[:, p, :], anum_d[p][tt * 128:(tt + 1) * 128, :], reads=[anum_d[p]], writes=[n3])
                S.dma(m3[:, p, :], am_d[p][tt * 128:(tt + 1) * 128, :], reads=[am_d[p]], writes=[m3])
                S.dma(d3[:, p, :], ad_d[p][tt * 128:(tt + 1) * 128, :], reads=[ad_d[p]], writes=[d3])
            S.op("dve", lambda e, mM=mM, m3=m3: e.tensor_tensor(out=mM[:, 0:8], in0=m3[:, 0, :], in1=m3[:, 1, :], op=ALU.max), reads=[m3], writes=[mM])
            S.op("dve", lambda e, mM=mM, m3=m3: e.tensor_tensor(out=mM[:, 0:8], in0=mM[:, 0:8], in1=m3[:, 2, :], op=ALU.max), reads=[m3, mM], writes=[mM])
            S.op("dve", lambda e, mM=mM, m3=m3, w3_=w3_: e.tensor_tensor(out=w3_[:, :, :], in0=m3[:, :, :], in1=mM[:, 0:8].unsqueeze(1).to_broadcast([128, 3, 8]), op=ALU.subtract), reads=[m3, mM], writes=[w3_])
            S.op("act", lambda e, w3_=w3_: e.activation(out=w3_[:, :, :], in_=w3_[:, :, :], func=AF.Exp), reads=[w3_], writes=[w3_])
            S.op("dve", lambda e, d3=d3, w3_=w3_: e.tensor_tensor(out=d3[:, :, :], in0=d3[:, :, :], in1=w3_[:, :, :], op=ALU.mult), reads=[d3, w3_], writes=[d3])
            S.op("dve", lambda e, d3=d3, mM=mM: e.tensor_tensor(out=mM[:, 8:16], in0=d3[:, 0, :], in1=d3[:, 1, :], op=ALU.add), reads=[d3], writes=[mM])
            S.op("dve", lambda e, d3=d3, mM=mM: e.tensor_tensor(out=mM[:, 8:16], in0=mM[:, 8:16], in1=d3[:, 2, :], op=ALU.add), reads=[d3, mM], writes=[mM])
            S.op("dve", lambda e, mM=mM: e.reciprocal(out=mM[:, 8:16], in_=mM[:, 8:16]), reads=[mM], writes=[mM])
            S.op("dve", lambda e, mM=mM, w3_=w3_: e.tensor_tensor(out=w3_[:, :, :], in0=w3_[:, :, :], in1=mM[:, 8:16].unsqueeze(1).to_broadcast([128, 3, 8]), op=ALU.mult), reads=[mM, w3_], writes=[w3_])
            for p in range(3):
                eng = ("dve", "pool", "dve")[p]
                S.op(eng, lambda e, n3=n3, w3_=w3_, p=p: e.tensor_tensor(out=n3[:, p, :].rearrange("p (h d) -> p h d", d=128), in0=n3[:, p, :].rearrange("p (h d) -> p h d", d=128),
                                                                  in1=w3_[:, p, :].unsqueeze(2).to_broadcast([128, 8, 128]), op=ALU.mult), reads=[n3, w3_], writes=[n3])
            S.op("pool", lambda e, n3=n3: e.tensor_tensor(out=n3[:, 0, :], in0=n3[:, 0, :], in1=n3[:, 1, :], op=ALU.add), reads=[n3], writes=[n3])
            S.op("dve", lambda e, n3=n3, ob=ob: e.tensor_tensor(out=ob[:, :], in0=n3[:, 0, :], in1=n3[:, 2, :], op=ALU.add), reads=[n3], writes=[ob])
            S.dma(cat_d[tt * 128:(tt + 1) * 128, 1024:2048], ob[:, :], reads=[ob], writes=[cat_d], q="act")
        dbg_dram("cat3", cat_d, [NTOK, 2048], BF16)
        if stop_after <= 3:
            return early_exit()
        S.pop()

        def layer_norm_tile(y, gBt, bBt, outt, st8, jk):
            S.op("dve", lambda e: e.reduce_sum(out=st8[:, 0:1], in_=y[:, :], axis=AX.X), reads=[y, st8], writes=[st8])
            S.op("dve", lambda e: e.tensor_scalar(out=st8[:, 0:1], in0=st8[:, 0:1], scalar1=-1.0 / 2048.0, scalar2=None, op0=ALU.mult), reads=[st8], writes=[st8])
            S.op("act", lambda e: e.activation(out=jk[:, :], in_=y[:, :], func=AF.Square, bias=st8[:, 0:1]), reads=[y, st8], writes=[jk])
            S.op("dve", lambda e: e.reduce_sum(out=st8[:, 1:2], in_=jk[:, :], axis=AX.X), reads=[jk, st8], writes=[st8])
            S.op("dve", lambda e: e.tensor_scalar(out=st8[:, 1:2], in0=st8[:, 1:2], scalar1=1.0 / 2048.0, scalar2=LN_EPS, op0=ALU.mult, op1=ALU.add), reads=[st8], writes=[st8])
            S.op("act", lambda e: e.activation(out=st8[:, 2:3], in_=st8[:, 1:2], func=AF.Ln), reads=[st8], writes=[st8])
            S.op("act", lambda e: e.activation(out=st8[:, 1:2], in_=st8[:, 2:3], func=AF.Exp, scale=-0.5), reads=[st8], writes=[st8])
            S.op("dve", lambda e: e.tensor_scalar(out=y[:, :], in0=y[:, :], scalar1=st8[:, 0:1], scalar2=st8[:, 1:2], op0=ALU.add, op1=ALU.mult), reads=[y, st8], writes=[y])
            S.op("pool", lambda e: e.tensor_tensor(out=y[:, :], in0=y[:, :], in1=gBt[:, :], op=ALU.mult), reads=[y, gBt], writes=[y])
            S.op("pool", lambda e: e.tensor_tensor(out=outt[:, :], in0=y[:, :], in1=bBt[:, :], op=ALU.add), reads=[y, bBt], writes=[outt])

        def transpose16(src_bf, dstT, psbf):
            for rnd in range(2):
                def trx(e, rnd=rnd, psbf=psbf):
                    for j in range(8):
                        k = rnd * 8 + j
                        r = e.transpose(out=psbf[:, j * 128:(j + 1) * 128], in_=src_bf[:, k * 128:(k + 1) * 128], identity=identb[:, :])
                    return r
                S.op("pe", trx, reads=[src_bf, identb], writes=[psbf])
                cast(evac_rr(), dstT[:, rnd * 8:(rnd + 1) * 8, :].rearrange("p k t -> p (k t)"), psbf[:, :], [psbf], [dstT])

        S.mark("4")
        S.push()
        PSF, _pb = psum_std()
        psbf = _pb[0]
        psf_rr = RR(PSF)
        x1_d = S.dram("x1_d", [NTOK, D], F32)
        x1b_d = S.dram("x1b_d", [NTOK, D], BF16)
        WB = S.sb([128, 16, 2048], BF16, name="WB")
        S.dma(WB[:, :, :], w_out_b[:, :].rearrange("(k p) c -> p k c", p=128), reads=[w_out_b], writes=[WB])
        gB1 = S.sb([128, 2048], F32, name="gB1")
        bB1 = S.sb([128, 2048], F32, name="bB1")
        S.dma(gB1[:, :], I["ln1_g"].to_broadcast([128, 2048]), writes=[gB1])
        S.dma(bB1[:, :], I["ln1_b"].to_broadcast([128, 2048]), writes=[bB1])
        cts = [S.sb([128, 2048], BF16, name="ct%d" % i) for i in range(2)]
        cTs = [S.sb([128, 16, 128], BF16, name="cT%d" % i) for i in range(2)]
        xos = [S.sb([128, 2048], F32, name="xo%d" % i) for i in range(2)]
        ys = [S.sb([128, 2048], F32, name="y%d" % i) for i in range(2)]
        jks = [S.sb([128, 2048], F32, name="jk%d" % i) for i in range(1)]
        st8s = [S.sb([128, 8], F32, name="st8_%d" % i) for i in range(2)]
        x1bs = [S.sb([128, 2048], BF16, name="x1b%d" % i) for i in range(2)]
        for tt in range(16):
            ct, cT, xo, y, st8, x1b = cts[tt % 2], cTs[tt % 2], xos[tt % 2], ys[tt % 2], st8s[tt % 2], x1bs[tt % 2]
            S.dma(ct[:, :], cat_d[tt * 128:(tt + 1) * 128, :], reads=[cat_d], writes=[ct])
            S.dma(xo[:, :], I["xown"][tt * 128:(tt + 1) * 128, :], writes=[xo])
            transpose16(ct, cT, psbf)
            for cc in range(4):
                ps = psf_rr()
                def mmw(e, ps=ps, cT=cT, cc=cc, WB=WB):
                    for k in range(16):
                        r = e.matmul(ps[:, :], lhsT=cT[:, k, :], rhs=WB[:, k, cc * 512:(cc + 1) * 512], start=(k == 0), stop=(k == 15))
                    return r
                S.op("pe", mmw, reads=[cT, WB], writes=[ps])
                S.op("dve", lambda e, ps=ps, y=y, xo=xo, cc=cc: e.scalar_tensor_tensor(out=y[:, cc * 512:(cc + 1) * 512], in0=xo[:, cc * 512:(cc + 1) * 512], scalar=ALPHA, in1=ps[:, :], op0=ALU.mult, op1=ALU.add),
                     reads=[ps, xo, y], writes=[y])
            layer_norm_tile(y, gB1, bB1, xo, st8, jks[0])
            S.dma(x1_d[tt * 128:(tt + 1) * 128, :], xo[:, :], reads=[xo], writes=[x1_d], q="act")
            cast("act", x1b[:, :], xo[:, :], [xo], [x1b])
            S.dma(x1b_d[tt * 128:(tt + 1) * 128, :], x1b[:, :], reads=[x1b], writes=[x1b_d], q="act")
        dbg_dram("x1", x1_d, [NTOK, D], F32)
        if stop_after <= 4:
            return early_exit()
        S.pop()

        S.mark("5A")
        S.push()
        P2 = [S.ps([128, 1024], F32) for i in range(2)]
        PG = [S.ps([128, 512], F32) for i in range(3)]
        psq = S.ps([128, 512], F32)
        psbf = Buf(psq[:, :].bitcast(BF16))
        psbf.writers = psq.writers
        psbf.readers = psq.readers
        Gd = S.dram("Gd", [NTOK, 16384], BF16)
        x1T_d = S.dram("x1T_d", [D, NTOK], BF16)
        WQ = S.sb([128, 16, 2048], BF16, name="WQ")
        S.dma(WQ[:, :, :], w_q_b[:, :].rearrange("(k p) c -> p k c", p=128), reads=[w_q_b], writes=[WQ])
        kf = S.sb([128, 2, 128], F32, name="kf")
        kb2 = S.sb([128, 2, 128], BF16, name="kb2")
        S.dma(kf[:, 0, :], I["k1T"], writes=[kf])
        S.dma(kf[:, 1, :], I["k2T"], writes=[kf])
        S.op("dve", lambda e: e.tensor_copy(out=kb2[:, :, :], in_=kf[:, :, :]), reads=[kf], writes=[kb2])
        PK1T = S.sb([128, 128, 128], BF16, name="PK1T")
        S.op("dve", lambda e: e.tensor_copy(out=PK1T[:, :, :], in_=kb2[:, 0, :].unsqueeze(2).to_broadcast([128, 128, 128])), reads=[kb2], writes=[PK1T])
        K2rep = S.sb([128, 4, 128], BF16, name="K2rep")
        S.op("dve", lambda e: e.tensor_copy(out=K2rep[:, :, :], in_=kb2[:, 1, :].unsqueeze(1).to_broadcast([128, 4, 128])), reads=[kb2], writes=[K2rep])
        PKf = PK1T[:, :, :].rearrange("p a b -> p (a b)")
        K2f = K2rep[:, :, :].rearrange("p a b -> p (a b)")
        xb1s = [S.sb([128, 2048], BF16, name="xb1_%d" % i) for i in range(2)]
        x1Ts = [S.sb([128, 16, 128], BF16, name="x1T_%d" % i) for i in range(2)]
        qTts = [S.sb([128, 16, 128], BF16, name="qTt_%d" % i) for i in range(2)]
        scs = [S.sb([128, 16, 128], F32, name="sc_%d" % i) for i in range(2)]
        tmpk = S.sb([128, 256], F32, name="tmpk")
        m16s = [S.sb([128, 16, 16], F32, name="m16_%d" % i) for i in range(2)]
        cand = S.sb([128, 8, 256], F32, name="cand")
        t16s = [S.sb([128, 8, 16], F32, name="t16_%d" % i) for i in range(2)]
        e16 = S.sb([128, 8, 16], F32, name="e16")
        tzs = [S.sb([128, 32], F32, name="tz_%d" % i) for i in range(2)]
        Es = [S.sb([128, 1024], BF16, name="E_%d" % i) for i in range(3)]
        Gms = [S.sb([128, 1024], BF16, name="Gm_%d" % i) for i in range(3)]
        Gts = [S.sb([128, 1024], BF16, name="Gt_%d" % i) for i in range(2)]
        BIGNEG = -1.0e30
        gi = 0
        deferred = []
        for tt in range(16):
            xb1, x1T, qTt, sc = xb1s[tt % 2], x1Ts[tt % 2], qTts[tt % 2], scs[tt % 2]
            m16, t16, tz = m16s[tt % 2], t16s[tt % 2], tzs[tt % 2]
            S.dma(xb1[:, :], x1b_d[tt * 128:(tt + 1) * 128, :], reads=[x1b_d], writes=[xb1])
            transpose16(xb1, x1T, psbf)
            S.dma(x1T_d[:, :].rearrange("(k p) t -> p k t", p=128)[:, :, tt * 128:(tt + 1) * 128], x1T[:, :, :], reads=[x1T], writes=[x1T_d], q="act")
            for q4 in range(4):
                def mmq(e, q4=q4, x1T=x1T, psq=psq, WQ=WQ):
                    for j in range(4):
                        qc = q4 * 4 + j
                        for k in range(16):
                            r = e.matmul(psq[:, j * 128:(j + 1) * 128], lhsT=WQ[:, k, qc * 128:(qc + 1) * 128], rhs=x1T[:, k, :], start=(k == 0), stop=(k == 15))
                    return r
                S.op("pe", mmq, reads=[WQ, x1T], writes=[psq])
                cast(evac_rr(), qTt[:, q4 * 4:(q4 + 1) * 4, :].rearrange("p a b -> p (a b)"), psq[:, :], [psq], [qTt])
            for q4 in range(4):
                def mms(e, q4=q4, qTt=qTt, psq=psq, kb2=kb2):
                    for j in range(4):
                        qc = q4 * 4 + j
                        r = e.matmul(psq[:, j * 128:(j + 1) * 128], lhsT=qTt[:, qc, :], rhs=kb2[:, qc % 2, :], start=True, stop=True)
                    return r
                S.op("pe", mms, reads=[qTt, kb2], writes=[psq])
                cast(evac_rr(), sc[:, q4 * 4:(q4 + 1) * 4, :].rearrange("p a b -> p (a b)"), psq[:, :], [psq], [sc])
            for qc in range(16):
                S.op("dve", lambda e, qc=qc, sc=sc, m16=m16: e.max(out=m16[:, qc, 0:8], in_=sc[:, qc, :]), reads=[sc, m16], writes=[m16])
                S.op("dve", lambda e, qc=qc, sc=sc, m16=m16: e.match_replace(out=tmpk[:, 0:128], in_to_replace=m16[:, qc, 0:8], in_values=sc[:, qc, :], imm_value=BIGNEG), reads=[sc, m16], writes=[tmpk])
                S.op("dve", lambda e, qc=qc, m16=m16: e.max(out=m16[:, qc, 8:16], in_=tmpk[:, 0:128]), reads=[tmpk, m16], writes=[m16])
            m16v = m16[:, :, :].rearrange("p (h two) k -> p h two k", two=2)
            S.op("dve", lambda e, m16v=m16v: e.tensor_tensor(out=cand[:, :, :].rearrange("p h (i j) -> p h i j", j=16), in0=m16v[:, :, 0, :].unsqueeze(3).to_broadcast([128, 8, 16, 16]),
                                                            in1=m16v[:, :, 1, :].unsqueeze(2).to_broadcast([128, 8, 16, 16]), op=ALU.add), reads=[m16], writes=[cand])
            for h in range(8):
                S.op("dve", lambda e, h=h, t16=t16: e.max(out=t16[:, h, 0:8], in_=cand[:, h, :]), reads=[cand, t16], writes=[t16])
                S.op("dve", lambda e, h=h, t16=t16: e.match_replace(out=tmpk[:, :], in_to_replace=t16[:, h, 0:8], in_values=cand[:, h, :], imm_value=BIGNEG), reads=[cand, t16], writes=[tmpk])
                S.op("dve", lambda e, h=h, t16=t16: e.max(out=t16[:, h, 8:16], in_=tmpk[:, :]), reads=[tmpk, t16], writes=[t16])
            S.op("dve", lambda e, t16=t16, tz=tz: e.tensor_scalar(out=tz[:, 0:8], in0=t16[:, :, 15], scalar1=-2.0e-5, scalar2=None, op0=ALU.add), reads=[t16, tz], writes=[tz])
            S.op("dve", lambda e, t16=t16: e.tensor_tensor(out=e16[:, :, :], in0=t16[:, :, :], in1=t16[:, :, 0:1].to_broadcast([128, 8, 16]), op=ALU.subtract), reads=[t16], writes=[e16])
            S.op("act", lambda e: e.activation(out=e16[:, :, :], in_=e16[:, :, :], func=AF.Exp), reads=[e16], writes=[e16])
            S.op("dve", lambda e, tz=tz: e.reduce_sum(out=tz[:, 8:16], in_=e16[:, :, :], axis=AX.X), reads=[e16, tz], writes=[tz])
            S.op("act", lambda e, tz=tz: e.activation(out=tz[:, 8:16], in_=tz[:, 8:16], func=AF.Ln), reads=[tz], writes=[tz])
            S.op("dve", lambda e, tz=tz, t16=t16: e.tensor_tensor(out=tz[:, 16:24], in0=tz[:, 8:16], in1=t16[:, :, 0], op=ALU.add), reads=[tz, t16], writes=[tz])
            S.op("dve", lambda e, tz=tz: e.tensor_scalar(out=tz[:, 16:24], in0=tz[:, 16:24], scalar1=-1.0, scalar2=None, op0=ALU.mult), reads=[tz], writes=[tz])
            for eg in range(16):
                Gt = Gts[eg % 2]
                pg = [PG[(2 * (tt * 16 + eg)) % 3], PG[(2 * (tt * 16 + eg) + 1) % 3]]
                for h in range(8):
                    psP = P2[gi % 2]
                    E = Es[gi % 3]
                    Gm = Gms[gi % 3]
                    gi += 1
                    def mmp(e, psP=psP, h=h, eg=eg, qTt=qTt, PKf=PKf, K2f=K2f):
                        for hf in range(2):
                            c0 = eg * 1024 + hf * 512
                            e.matmul(psP[:, hf * 512:(hf + 1) * 512], lhsT=qTt[:, 2 * h, :], rhs=PKf[:, c0:c0 + 512], start=True, stop=False)
                            r = e.matmul(psP[:, hf * 512:(hf + 1) * 512], lhsT=qTt[:, 2 * h + 1, :], rhs=K2f, start=False, stop=True)
                        return r
                    S.op("pe", mmp, reads=[qTt, PK1T, K2rep], writes=[psP])
                    for fn_ in deferred:
                        fn_()
                    deferred = []
                    S.op("act", lambda e, psP=psP, E=E, h=h, tz=tz: e.activation(out=E[:, :], in_=psP[:, :], func=AF.Exp, bias=tz[:, 16 + h:17 + h]), reads=[psP, tz], writes=[E])
                    S.op("dve", lambda e, psP=psP, E=E, Gm=Gm, h=h, tz=tz: e.scalar_tensor_tensor(out=Gm[:, :], in0=psP[:, :], scalar=tz[:, h:h + 1], in1=E[:, :], op0=ALU.is_ge, op1=ALU.mult),
                         reads=[psP, E, tz], writes=[Gm])
                    def do_mmg(Gm=Gm, pg=pg, h=h):
                        def mmg(e):
                            for hf in range(2):
                                r = e.matmul(pg[hf][:, :], lhsT=identb[:, :], rhs=Gm[:, hf * 512:(hf + 1) * 512], start=(h == 0), stop=(h == 7))
                            return r
                        S.op("pe", mmg, reads=[Gm, identb], writes=pg)
                    deferred.append(do_mmg)
                def do_evac(Gt=Gt, pg=pg, tt=tt, eg=eg):
                    for hf in range(2):
                        cast("act" if hf == 0 else "dve", Gt[:, hf * 512:(hf + 1) * 512], pg[hf][:, :], [pg[hf]], [Gt])
                    S.dma(Gd[tt * 128:(tt + 1) * 128, eg * 1024:(eg + 1) * 1024], Gt[:, :], reads=[Gt], writes=[Gd], q="act")
                deferred.append(do_evac)
        for fn_ in deferred:
            fn_()
        deferred = []
        dbg_dram("Gd", Gd, [NTOK, 16384], BF16)
        if stop_after <= 5:
            return early_exit()
        S.pop()

        S.mark("5B")
        S.push()
        PSF, PBF = psum_std(6, 2)
        psf_rr = RR(PSF)
        actT_d = S.dram("actT_d", [16384, NTOK], BF16)
        xTs = [S.sb([128, 16, 512], BF16, name="xTs%d" % i) for i in range(2)]
        uts = [S.sb([128, 16, 512], BF16, name="ut%d" % i) for i in range(2)]
        gls = [S.sb([128, 512], BF16, name="gl%d" % i) for i in range(3)]
        gts = [S.sb([128, 512], BF16, name="gtl%d" % i) for i in range(4)]
        aTs = [S.sb([128, 4, 128], BF16, name="aT%d" % i) for i in range(3)]
        uT3 = uT_b[:, :].rearrange("(k p) c -> p k c", p=128)
        aT3 = actT_d[:, :].rearrange("(c p) t -> p c t", p=128)
        gi = 0
        for stile in range(4):
            xT = xTs[stile % 2]
            S.dma(xT[:, :, :], x1T_d[:, :].rearrange("(k p) t -> p k t", p=128)[:, :, stile * 512:(stile + 1) * 512], reads=[x1T_d], writes=[xT])
            for ec in range(32):
                ut = uts[ec % 2]
                S.dma(ut[:, :, :], uT3[:, :, ec * 512:(ec + 1) * 512], reads=[uT_b], writes=[ut])
                for sub in range(4):
                    tok0 = stile * 512 + sub * 128
                    gt_ = gts[gi % 4]
                    gl = gls[gi % 3]
                    aT = aTs[gi % 3]
                    pb = PBF[gi % 2]
                    gi += 1
                    S.dma(gt_[:, :], Gd[tok0:tok0 + 128, ec * 512:(ec + 1) * 512], reads=[Gd], writes=[gt_])
                    ps = psf_rr()
                    def mmh(e, ps=ps, xT=xT, ut=ut, sub=sub):
                        for k in range(16):
                            r = e.matmul(ps[:, :], lhsT=xT[:, k, sub * 128:(sub + 1) * 128], rhs=ut[:, k, :], start=(k == 0), stop=(k == 15))
                        return r
                    S.op("pe", mmh, reads=[xT, ut], writes=[ps])
                    S.op("act", lambda e, ps=ps, gl=gl: e.activation(out=gl[:, :], in_=ps[:, :], func=AF.Gelu), reads=[ps], writes=[gl])
                    S.op("dve", lambda e, gl=gl, gt_=gt_: e.tensor_tensor(out=gl[:, :], in0=gl[:, :], in1=gt_[:, :], op=ALU.mult), reads=[gl, gt_], writes=[gl])
                    def tra(e, gl=gl, pb=pb):
                        for j in range(4):
                            r = e.transpose(out=pb[:, j * 128:(j + 1) * 128], in_=gl[:, j * 128:(j + 1) * 128], identity=identb[:, :])
                        return r
                    S.op("pe", tra, reads=[gl, identb], writes=[pb])
                    S.op("dve", lambda e, aT=aT, pb=pb: e.tensor_copy(out=aT[:, :, :].rearrange("p a b -> p (a b)"), in_=pb[:, 0:512]), reads=[pb], writes=[aT])
                    S.dma(aT3[:, ec * 4:(ec + 1) * 4, tok0:tok0 + 128], aT[:, :, :], reads=[aT], writes=[actT_d], q="act")
        if stop_after <= 6:
            return early_exit()
        S.pop()

        S.mark("5C")
        S.push()
        PS8 = [S.ps([128, 512], F32) for i in range(8)]
        gB2 = S.sb([128, 2048], F32, name="gB2")
        bB2 = S.sb([128, 2048], F32, name="bB2")
        S.dma(gB2[:, :], I["ln2_g"].to_broadcast([128, 2048]), writes=[gB2])
        S.dma(bB2[:, :], I["ln2_b"].to_broadcast([128, 2048]), writes=[bB2])
        vvs = [S.sb([128, 2048], BF16, name="vv%d" % i) for i in range(4)]
        ats = [S.sb([128, 256], BF16, name="at%d" % i) for i in range(4)]
        x1s = [S.sb([128, 2048], F32, name="x1s%d" % i) for i in range(2)]
        y2s = [S.sb([128, 2048], F32, name="y2s%d" % i) for i in range(2)]
        jk2 = S.sb([128, 2048], F32, name="jk2")
        st82 = [S.sb([128, 8], F32, name="st82_%d" % i) for i in range(2)]
        for tg in range(8):
            for ec in range(128):
                vv = vvs[ec % 4]
                at = ats[ec % 4]
                S.dma(vv[:, :], ev_b[ec * 128:(ec + 1) * 128, :], reads=[ev_b], writes=[vv])
                S.dma(at[:, :], actT_d[ec * 128:(ec + 1) * 128, tg * 256:(tg + 1) * 256], reads=[actT_d], writes=[at])
                def mmv(e, vv=vv, at=at, ec=ec, PS8=PS8):
                    for hf in range(2):
                        for j in range(4):
                            r = e.matmul(PS8[hf * 4 + j][:, :], lhsT=at[:, hf * 128:(hf + 1) * 128], rhs=vv[:, j * 512:(j + 1) * 512], start=(ec == 0), stop=(ec == 127))
                    return r
                S.op("pe", mmv, reads=[vv, at], writes=PS8)
            for hf in range(2):
                tt = tg * 2 + hf
                x1t, y2, st8 = x1s[hf], y2s[hf], st82[hf]
                S.dma(x1t[:, :], x1_d[tt * 128:(tt + 1) * 128, :], reads=[x1_d], writes=[x1t])
                for j in range(4):
                    pj = PS8[hf * 4 + j]
                    S.op("dve", lambda e, j=j, y2=y2, x1t=x1t, pj=pj: e.scalar_tensor_tensor(out=y2[:, j * 512:(j + 1) * 512], in0=x1t[:, j * 512:(j + 1) * 512], scalar=ALPHA, in1=pj[:, :], op0=ALU.mult, op1=ALU.add),
                         reads=[pj, x1t, y2], writes=[y2])
                layer_norm_tile(y2, gB2, bB2, x1t, st8, jk2)
                S.dma(out_ap[tt * 128:(tt + 1) * 128, :], x1t[:, :], reads=[x1t], writes=[OUT], q="act")
        S.mark("end")
        nc._marks = S.marks
        S.finish([OUT] + dbg_bufs)
        S.emit()
        S.pop_all()
        return nc


def make_in_maps(inp):
    x = np.asarray(inp["x"], np.float32)
    shared = {
        "w_in": np.ascontiguousarray(inp["w_in"][0]),
        "conv_wT": np.ascontiguousarray(inp["conv_w"][0].T.reshape(16, 128, 4).transpose(1, 0, 2)),
        "conv_b": np.ascontiguousarray(inp["conv_b"][0].reshape(16, 128).T),
        "b_ig": np.ascontiguousarray(inp["b_igate"][0].reshape(4, 1)),
        "b_fg": np.ascontiguousarray(inp["b_fgate"][0].reshape(4, 1)),
        "mh_g": np.ascontiguousarray(inp["mh_norm_g"][0].reshape(1, 1024)),
        "w_out": np.ascontiguousarray(inp["w_out"][0]),
        "ln1_g": np.ascontiguousarray(inp["ln1_g"][0].reshape(1, D)),
        "ln1_b": np.ascontiguousarray(inp["ln1_b"][0].reshape(1, D)),
        "w_q": np.ascontiguousarray(inp["w_query"][0]),
        "k1T": np.ascontiguousarray(inp["sub_keys_1"][0].T),
        "k2T": np.ascontiguousarray(inp["sub_keys_2"][0].T),
        "uT": np.ascontiguousarray(inp["expert_u"][0].T),
        "ev": np.ascontiguousarray(inp["expert_v"][0]),
        "ln2_g": np.ascontiguousarray(inp["ln2_g"][0].reshape(1, D)),
        "ln2_b": np.ascontiguousarray(inp["ln2_b"][0].reshape(1, D)),
        "ident": np.eye(128, dtype=np.float32),
    }
    shared = {k: v.astype(np.float32, copy=False) for k, v in shared.items()}
    jj = np.arange(128)[:, None]
    cc = np.arange(256)[None, :]
    maskA = np.where((cc >= jj) & (cc <= jj + 128), 0.0, NEG).astype(np.float32)
    maskC0 = maskA.copy()
    maskC0[:, 0:128] = NEG
    ss = np.arange(64)[:, None]
    tt = np.arange(64)[None, :]
    cmask = (tt >= ss).astype(np.float32)
    shared["maskA"] = maskA
    shared["cmask"] = cmask
    half = 16
    inv = (500000.0 ** (-np.arange(half, dtype=np.float32) / half)).astype(np.float32)
    maps = []
    for c in range(8):
        b, h = c // 2, c % 2
        m = dict(shared)
        xT = np.zeros((D, NALL), np.float32)
        own = x[b, h * 2048:(h + 1) * 2048]
        xT[:, 2048:] = own.T
        if h == 1:
            xT[:, :2048] = x[b, 0:2048].T
        m["xT"] = xT
        m["xown"] = np.ascontiguousarray(own)
        m["flag"] = np.full((128, 1), float(h), np.float32)
        pos = (np.arange(NALL, dtype=np.float32) + (h - 1) * 2048.0).astype(np.float32)
        ang = pos[:, None] * inv[None, :]
        m["cs"] = np.concatenate([np.cos(ang), np.sin(ang)], axis=1).astype(np.float32)
        m["maskC"] = maskA if h == 1 else maskC0
        maps.append(m)
    return maps


_NC_CACHE = {}


def kernel(**inputs):
    maps = make_in_maps(inputs)
    if "nc" not in _NC_CACHE:
        _NC_CACHE["nc"] = build_program()
    nc = _NC_CACHE["nc"]
    res = run_bass_kernel_spmd(nc, maps, core_ids=list(range(8)))
    out = np.zeros((4, 4096, D), np.float32)
    for c in range(8):
        b, h = c // 2, c % 2
        out[b, h * 2048:(h + 1) * 2048] = res.results[c]["out"]
    return out
```

```python
import contextlib
import numpy as np
import concourse.bass as bass
import concourse.mybir as mybir
from concourse.bass_utils import run_bass_kernel_spmd

F32 = mybir.dt.float32
BF16 = mybir.dt.bfloat16
ALU = mybir.AluOpType
AF = mybir.ActivationFunctionType
AX = mybir.AxisListType

NSLOT = 8
D = 2048
NTOK = 2048
NALL = 4096
INC = 7176
ALPHA = 2.0 ** 0.25
LN_EPS = 1e-5
NEG = -30000.0


class Buf:
    def __init__(self, t, disjoint=False):
        self.t = t
        self.disjoint = disjoint
        self.writers = {}
        self.readers = {}

    def __getitem__(self, idx):
        return self.t[idx]


class Sched:
    ENGS = ("pe", "act", "dve", "pool", "sp")

    def __init__(self, nc, stack):
        self.nc = nc
        self.stack = stack
        self.ops = {e: [] for e in self.ENGS}
        self.sems = {}
        self.cnt = {}
        self.seen = {e: {} for e in self.ENGS}
        for e in ("pe", "act", "dve", "pool"):
            self.sems[e] = stack.enter_context(nc.semaphore("s_" + e))
            self.cnt[e] = 0
        for q in ("sp", "act", "pool"):
            for s in range(NSLOT):
                k = ("dma", q, s)
                self.sems[k] = stack.enter_context(nc.semaphore("d_%s%d" % (q, s)))
                self.cnt[k] = 0
        self.dma_rr = {"sp": 0, "act": 0, "pool": 0}
        self.nbuf = 0
        self.stacks = []
        self.marks = []

    def sb(self, shape, dt=F32, name=None):
        self.nbuf += 1
        t = self.stack.enter_context(self.nc.sbuf_tensor("sb_" + (name or ("%d" % self.nbuf)), list(shape), dt))
        return Buf(t)

    def ps(self, shape, dt=F32, name=None):
        self.nbuf += 1
        t = self.stack.enter_context(self.nc.psum_tensor("ps_" + (name or ("%d" % self.nbuf)), list(shape), dt))
        return Buf(t)

    def dram(self, name, shape, dt=F32):
        t = self.nc.dram_tensor(name, list(shape), dt, kind="Internal")
        return Buf(t, disjoint=True)

    def _need(self, eng, reads, writes):
        need = {}

        def add(d, skip_own=False):
            for c, v in d.items():
                if skip_own and c == eng:
                    continue
                if need.get(c, 0) < v:
                    need[c] = v

        for b in reads:
            add(b.writers)
        for b in writes:
            if not b.disjoint:
                add(b.writers)
            add(b.readers)
        out = []
        seen = self.seen[eng]
        for c, v in need.items():
            if seen.get(c, 0) < v:
                seen[c] = v
                out.append((c, v))
        return out

    def _commit(self, clock, val, reads, writes):
        for b in reads:
            if b.readers.get(clock, 0) < val:
                b.readers[clock] = val
        for b in writes:
            if b.disjoint:
                if b.writers.get(clock, 0) < val:
                    b.writers[clock] = val
            else:
                b.writers.clear()
                b.writers[clock] = val
                b.readers.clear()

    def op(self, eng, fn, reads=(), writes=()):
        waits = self._need(eng, reads, writes)
        self.cnt[eng] += 1
        val = self.cnt[eng]
        self.ops[eng].append((waits, fn, eng, 1))
        self._commit(eng, val, reads, writes)

    def dma(self, out_ap, in_ap, reads=(), writes=(), q="sp", **kw):
        s = self.dma_rr[q]
        self.dma_rr[q] = (s + 1) % NSLOT
        clock = ("dma", q, s)
        waits = self._need(q, reads, writes)
        prev = self.cnt[clock]
        if prev > 0 and self.seen[q].get(clock, 0) < prev:
            self.seen[q][clock] = prev
            waits.append((clock, prev))
        self.cnt[clock] += 16
        val = self.cnt[clock]

        def fn(e, out_ap=out_ap, in_ap=in_ap, kw=kw):
            return e.dma_start(out=out_ap, in_=in_ap, **kw)

        self.ops[q].append((waits, fn, clock, 16))
        self._commit(clock, val, reads, writes)

    def mark(self, name):
        self.marks.append((name, dict(self.cnt)))

    def push(self):
        st = contextlib.ExitStack()
        self.stacks.append(self.stack)
        self.stack = st

    def barrier(self):
        for e in self.ENGS:
            waits = []
            for c, v in self.cnt.items():
                if v > 0 and self.seen[e].get(c, 0) < v:
                    self.seen[e][c] = v
                    waits.append((c, v))
            self.ops[e].append((waits, None, None, 0))

    def pop(self, barrier=True):
        if barrier:
            self.barrier()
        self.stack.close()
        self.stack = self.stacks.pop()

    def pop_all(self):
        while self.stacks:
            self.pop(barrier=False)

    def finish(self, final_bufs):
        waits = self._need("sp", final_bufs, ())
        self.ops["sp"].append((waits, None, None, 0))

    def emit(self):
        nc = self.nc
        sems = self.sems

        def run(engname):
            def body(e):
                for waits, fn, clock, inc in self.ops[engname]:
                    for c, v in waits:
                        e.wait_ge(sems[c], v)
                    if fn is not None:
                        ins = fn(e)
                        ins.then_inc(sems[clock], inc)
            return body

        with nc.Block() as block:
            block.tensor(run("pe"))
            block.scalar(run("act"))
            block.vector(run("dve"))
            block.gpsimd(run("pool"))
            block.sync(run("sp"))


class RR:
    def __init__(self, items):
        self.items = list(items)
        self.i = 0

    def __call__(self):
        x = self.items[self.i % len(self.items)]
        self.i += 1
        return x


INPUT_SPECS = [
    ("xT", [D, NALL], F32), ("xown", [NTOK, D], F32),
    ("w_in", [D, INC], F32), ("conv_wT", [128, 16, 4], F32), ("conv_b", [128, 16], F32),
    ("b_ig", [4, 1], F32), ("b_fg", [4, 1], F32), ("mh_g", [1, 1024], F32),
    ("w_out", [D, D], F32), ("ln1_g", [1, D], F32), ("ln1_b", [1, D], F32),
    ("w_q", [D, D], F32), ("k1T", [128, 128], F32), ("k2T", [128, 128], F32),
    ("uT", [D, 16384], F32), ("ev", [16384, D], F32), ("ln2_g", [1, D], F32), ("ln2_b", [1, D], F32),
    ("flag", [128, 1], F32), ("cs", [NALL, 32], F32), ("maskA", [128, 256], F32), ("maskC", [128, 256], F32),
    ("ident", [128, 128], F32), ("cmask", [64, 64], F32),
]


def build_program(stop_after=99, dbg=()):
    nc = bass.Bass("TRN2", target_bir_lowering=False)
    I = {}
    for name, shape, dt in INPUT_SPECS:
        if stop_after < 5 and name in ("uT", "ev"):
            continue
        I[name] = nc.dram_tensor(name, shape, dt, kind="ExternalInput").ap()
    out_ap = nc.dram_tensor("out", [NTOK, D], F32, kind="ExternalOutput").ap()
    OUT = Buf(out_ap, disjoint=True)
    dbg_bufs = []

    with contextlib.ExitStack() as st:
        S = Sched(nc, st)
        P = lambda **k: None

        def dbg_out(name, src_buf, shape, dt):
            if name in dbg:
                o = nc.dram_tensor("dbg_" + name, list(shape), dt, kind="ExternalOutput").ap()
                ob = Buf(o, disjoint=True)
                S.dma(o, src_buf[:], reads=[src_buf], writes=[ob])
                dbg_bufs.append(ob)

        w_in_b = S.dram("w_in_b", [D, INC], BF16)
        w_out_b = S.dram("w_out_b", [D, D], BF16)
        w_q_b = S.dram("w_q_b", [D, D], BF16)
        uT_b = S.dram("uT_b", [D, 16384], BF16)
        ev_b = S.dram("ev_b", [16384, D], BF16)
        mqkT = S.dram("mqkT", [2048, NALL], F32)
        gT = S.dram("gT", [8, NALL], F32)
        mv_d = S.dram("mv_d", [NALL, 1024], F32)
        mo_d = S.dram("mo_d", [NALL, 1024], F32)
        aq_d = S.dram("aq_d", [NALL, 1024], F32)
        ak_d = S.dram("ak_d", [NALL, 1024], F32)
        av_d = S.dram("av_d", [NALL, 1024], BF16)

        def psum_std(nf=7, nb=1):
            F_ = [S.ps([128, 512], F32) for i in range(nf)]
            B_ = [S.ps([128, 1024], BF16) for i in range(nb)]
            return F_, B_

        identf = S.sb([128, 128], F32, name="identf")
        identb = S.sb([128, 128], BF16, name="identb")
        S.dma(identf[:, :], I["ident"], writes=[identf])
        S.op("dve", lambda e: e.tensor_copy(out=identb[:, :], in_=identf[:, :]), reads=[identf], writes=[identb])
        flag = S.sb([128, 1], F32, name="flag")
        S.dma(flag[:, :], I["flag"], writes=[flag])
        cmask = S.sb([64, 64], F32, name="cmask")
        S.dma(cmask[:, :], I["cmask"], writes=[cmask])
        scal = S.sb([64, 64, 12], F32, name="scal")
        decB = S.sb([128, 4, 64], F32, name="decB")

        S.push()
        PSF, _pb = psum_std()
        psbf = _pb[0]
        psf_rr = RR(PSF)
        ps_rr = psf_rr
        stg_f = [S.sb([128, 2048], F32, name="stgf%d" % i) for i in range(3)]
        stg_b = [S.sb([128, 2048], BF16, name="stgb%d" % i) for i in range(3)]
        cast_rr = RR(["dve", "act", "pool"])

        def cast(eng, out_ap, in_ap, reads, writes):
            if eng == "act":
                S.op("act", lambda e: e.activation(out=out_ap, in_=in_ap, func=AF.Copy), reads=reads, writes=writes)
            else:
                S.op(eng, lambda e: e.tensor_copy(out=out_ap, in_=in_ap), reads=reads, writes=writes)

        def convert(src_ap, dst, R, C):
            i = 0
            for r0 in range(0, R, 128):
                for c0 in range(0, C, 2048):
                    cc = min(2048, C - c0)
                    f = stg_f[i % 3]
                    b = stg_b[i % 3]
                    i += 1
                    S.dma(f[:, 0:cc], src_ap[r0:r0 + 128, c0:c0 + cc], writes=[f])
                    cast(cast_rr(), b[:, 0:cc], f[:, 0:cc], [f], [b])
                    S.dma(dst[r0:r0 + 128, c0:c0 + cc], b[:, 0:cc], reads=[b], writes=[dst], q="act")

        convert(I["w_in"], w_in_b, D, INC)
        convert(I["w_out"], w_out_b, D, D)
        convert(I["w_q"], w_q_b, D, D)

        S.mark("1")
        xf = [S.sb([128, 16, 512], F32, name="xf%d" % i) for i in range(1)]
        xb = [S.sb([128, 16, 512], BF16, name="xb%d" % i) for i in range(2)]
        wch = [S.sb([128, 16, 512], BF16, name="wch%d" % i) for i in range(2)]
        evo = [S.sb([128, 512], F32, name="evo%d" % i) for i in range(3)]
        evb = [S.sb([128, 512], BF16, name="evb%d" % i) for i in range(2)]
        xT3 = I["xT"].rearrange("(k p) t -> p k t", p=128)
        w3 = w_in_b[:, :].rearrange("(k p) c -> p k c", p=128)
        evac_rr = RR(["dve", "act"])
        it = 0
        for stile in range(8):
            t0 = stile * 512
            xbt = xb[stile % 2]
            for k in range(16):
                S.dma(xf[0][:, k, :], xT3[:, k, t0:t0 + 512], writes=[xf[0]])
            for k4 in range(4):
                cast(cast_rr(), xbt[:, k4 * 4:(k4 + 1) * 4, :], xf[0][:, k4 * 4:(k4 + 1) * 4, :], [xf[0]], [xbt])
            own = stile >= 4
            for cch in range(15):
                c0 = cch * 512
                cw = min(512, INC - c0)
                grp = c0 // 1024
                if c0 < 1024 and stile < 3:
                    continue
                if 3072 <= c0 < 4096 and not own:
                    continue
                wt = wch[it % 2]
                it += 1
                S.dma(wt[:, :, 0:cw], w3[:, :, c0:c0 + cw], reads=[w_in_b], writes=[wt])
                if c0 < 2048:
                    for sc in range(4):
                        ps = ps_rr()
                        def mm(e, ps=ps, wt=wt, sc=sc, xbt=xbt):
                            for k in range(16):
                                r = e.matmul(ps[:, :], lhsT=wt[:, k, sc * 128:(sc + 1) * 128], rhs=xbt[:, k, :],
                                             start=(k == 0), stop=(k == 15))
                            return r
                        S.op("pe", mm, reads=[wt, xbt], writes=[ps])
                        eo = evo[it % 3]
                        it += 1
                        cast(evac_rr(), eo[:, :], ps[:, :], [ps], [eo])
                        f0 = c0 + sc * 128
                        S.dma(mqkT[f0:f0 + 128, t0:t0 + 512], eo[:, :], reads=[eo], writes=[mqkT], q="act")
                else:
                    segs = []
                    for (lo, hi, dst, isbf) in ((2048, 3072, mv_d, False), (3072, 4096, mo_d, False),
                                                (4104, 5128, aq_d, False), (5128, 6152, ak_d, False),
                                                (6152, 7176, av_d, True)):
                        a = max(lo, c0)
                        b = min(hi, c0 + cw)
                        if a < b:
                            if dst is aq_d and not own:
                                continue
                            segs.append((a, b, dst, lo, isbf))
                    if c0 <= 4096 < c0 + cw:
                        ps = ps_rr()
                        g0 = 4096 - c0
                        def mmg(e, ps=ps, wt=wt, g0=g0, xbt=xbt):
                            for k in range(16):
                                r = e.matmul(ps[0:8, :], lhsT=wt[:, k, g0:g0 + 8], rhs=xbt[:, k, :],
                                             start=(k == 0), stop=(k == 15))
                            return r
                        S.op("pe", mmg, reads=[wt, xbt], writes=[ps])
                        eo = evo[it % 3]
                        it += 1
                        cast(evac_rr(), eo[0:8, :], ps[0:8, :], [ps], [eo])
                        S.dma(gT[0:8, t0:t0 + 512], eo[0:8, :], reads=[eo], writes=[gT], q="act")
                    for tt in range(4):
                        for (a, b, dst, lo, isbf) in segs:
                            ps = ps_rr()
                            n = b - a
                            def mmt(e, ps=ps, wt=wt, a=a, n=n, xbt=xbt, tt=tt, c0=c0):
                                for k in range(16):
                                    r = e.matmul(ps[:, 0:n], lhsT=xbt[:, k, tt * 128:(tt + 1) * 128],
                                                 rhs=wt[:, k, a - c0:a - c0 + n], start=(k == 0), stop=(k == 15))
                                return r
                            S.op("pe", mmt, reads=[wt, xbt], writes=[ps])
                            eo = (evb if isbf else evo)[it % 2]
                            it += 1
                            cast(evac_rr(), eo[:, 0:n], ps[:, 0:n], [ps], [eo])
                            r0 = t0 + tt * 128
                            S.dma(dst[r0:r0 + 128, a - lo:b - lo], eo[:, 0:n], reads=[eo], writes=[dst], q="act")

        for nm, bf, shp, dt in (("mqkT", mqkT, [2048, NALL], F32), ("gT", gT, [8, NALL], F32),
                                ("mv", mv_d, [NALL, 1024], F32), ("mo", mo_d, [NALL, 1024], F32),
                                ("aq", aq_d, [NALL, 1024], F32), ("ak", ak_d, [NALL, 1024], F32),
                                ("av", av_d, [NALL, 1024], BF16)):
            if nm in dbg:
                o = nc.dram_tensor("dbg_" + nm, shp, dt, kind="ExternalOutput").ap()
                ob = Buf(o, disjoint=True)
                S.dma(o, bf[:], reads=[bf], writes=[ob])
                dbg_bufs.append(ob)

        def dbg_dram(nm, bf, shp, dt):
            if nm in dbg:
                o = nc.dram_tensor("dbg_" + nm, list(shp), dt, kind="ExternalOutput").ap()
                ob = Buf(o, disjoint=True)
                S.dma(o, bf[:], reads=[bf], writes=[ob])
                dbg_bufs.append(ob)

        def early_exit():
            S.dma(out_ap[0:128, :], I["xown"][0:128, :], writes=[OUT])
            S.finish([OUT] + dbg_bufs)
            S.emit()
            S.pop_all()
            return nc

        if stop_after <= 1:
            return early_exit()

        S.pop()

        S.mark("2a")
        def v3(buf):
            return buf[:, :].rearrange("p (c t) -> p c t", t=64)

        S.push()
        PSF, _pb = psum_std()
        psbf = _pb[0]
        psf_rr = RR(PSF)
        ig = S.sb([4, NALL], F32, name="ig")
        fgt = S.sb([4, NALL], F32, name="fgt")
        S.dma(ig[:, :], gT[0:4, :], reads=[gT], writes=[ig])
        S.dma(fgt[:, :], gT[4:8, :], reads=[gT], writes=[fgt])
        big = S.sb([4, 1], F32, name="big")
        bfg = S.sb([4, 1], F32, name="bfg")
        S.dma(big[:, :], I["b_ig"], writes=[big])
        S.dma(bfg[:, :], I["b_fg"], writes=[bfg])
        nbf = S.sb([4, 1], F32, name="nbf")
        S.op("dve", lambda e: e.tensor_scalar(out=nbf[:, :], in0=bfg[:, :], scalar1=-1.0, scalar2=None, op0=ALU.mult),
             reads=[bfg], writes=[nbf])
        ga = S.sb([4, NALL], F32, name="ga")
        gb = fgt
        S.op("act", lambda e: e.activation(out=ga[:, :], in_=fgt[:, :], func=AF.Exp, bias=nbf[:, 0:1], scale=-1.0),
             reads=[fgt, nbf], writes=[ga])
        S.op("act", lambda e: e.activation(out=ga[:, :], in_=ga[:, :], func=AF.Ln, bias=1.0, scale=1.0),
             reads=[ga], writes=[ga])

        def logstep(src, tmp, op):
            cur, nxt = src, tmp
            for s_ in (1, 2, 4, 8, 16, 32):
                c3, n3 = v3(cur), v3(nxt)
                S.op("dve", lambda e, c3=c3, n3=n3, s_=s_: e.tensor_tensor(out=n3[:, :, s_:], in0=c3[:, :, s_:], in1=c3[:, :, :64 - s_], op=op),
                     reads=[cur], writes=[nxt])
                S.op("pool", lambda e, c3=c3, n3=n3, s_=s_: e.tensor_copy(out=n3[:, :, :s_], in_=c3[:, :, :s_]),
                     reads=[cur], writes=[nxt])
                cur, nxt = nxt, cur
            return cur

        cs_ = logstep(ga, gb, ALU.add)
        gg = S.sb([4, NALL], F32, name="gg")
        gg2 = ig
        S.op("dve", lambda e: e.scalar_tensor_tensor(out=gg[:, :], in0=ig[:, :], scalar=big[:, 0:1], in1=cs_[:, :], op0=ALU.add, op1=ALU.add),
             reads=[ig, big, cs_], writes=[gg])
        gsave = S.sb([4, NALL], F32, name="gsave")
        S.op("pool", lambda e: e.tensor_copy(out=gsave[:, :], in_=gg[:, :]), reads=[gg], writes=[gsave])
        cm = logstep(gg, gg2, ALU.max)
        mst = S.sb([4, 65], F32, name="mst")
        Mc = S.sb([4, 64], F32, name="Mc")
        S.op("dve", lambda e: e.memset(mst[:, :], 0.0), writes=[mst])
        cm3 = v3(cm)
        cs3 = v3(cs_)
        for c in range(64):
            if c == 32:
                S.op("dve", lambda e: e.tensor_tensor(out=mst[:, 32:33], in0=mst[:, 32:33], in1=flag[0:4, 0:1], op=ALU.mult),
                     reads=[mst, flag], writes=[mst])
            S.op("dve", lambda e, c=c: e.tensor_tensor(out=Mc[:, c:c + 1], in0=mst[:, c:c + 1], in1=cm3[:, c, 63:64], op=ALU.max),
                 reads=[mst, cm], writes=[Mc])
            S.op("dve", lambda e, c=c: e.tensor_tensor(out=mst[:, c + 1:c + 2], in0=Mc[:, c:c + 1], in1=cs3[:, c, 63:64], op=ALU.subtract),
                 reads=[Mc, cs_], writes=[mst])
        MT = S.sb([4, NALL], F32, name="MT")
        bc_m = lambda b_: b_[:, 0:64].unsqueeze(2).to_broadcast([4, 64, 64])
        S.op("dve", lambda e: e.tensor_tensor(out=v3(MT), in0=cm3, in1=bc_m(mst), op=ALU.max), reads=[cm, mst], writes=[MT])
        stk = S.sb([12, NALL], F32, name="stk")
        tq = [S.sb([4, NALL], F32, name="tq%d" % i) for i in range(3)]
        S.op("dve", lambda e: e.tensor_tensor(out=v3(tq[0]), in0=v3(gsave), in1=bc_m(Mc), op=ALU.subtract), reads=[gsave, Mc], writes=[tq[0]])
        S.op("act", lambda e: e.activation(out=tq[0][:, :], in_=tq[0][:, :], func=AF.Exp), reads=[tq[0]], writes=[tq[0]])
        S.op("dve", lambda e: e.tensor_tensor(out=v3(tq[1]), in0=bc_m(Mc), in1=v3(MT), op=ALU.subtract), reads=[MT, Mc], writes=[tq[1]])
        S.op("act", lambda e: e.activation(out=tq[1][:, :], in_=tq[1][:, :], func=AF.Exp), reads=[tq[1]], writes=[tq[1]])
        S.op("dve", lambda e: e.tensor_tensor(out=tq[2][:, :], in0=cs_[:, :], in1=MT[:, :], op=ALU.subtract), reads=[MT, cs_], writes=[tq[2]])
        S.op("act", lambda e: e.activation(out=tq[2][:, :], in_=tq[2][:, :], func=AF.Exp), reads=[tq[2]], writes=[tq[2]])
        for i in range(3):
            S.dma(stk[4 * i:4 * i + 4, :], tq[i][:, :], reads=[tq[i]], writes=[stk])
        dec = S.sb([4, 64], F32, name="dec")
        S.op("dve", lambda e: e.tensor_tensor(out=dec[:, :], in0=mst[:, 0:64], in1=Mc[:, :], op=ALU.subtract), reads=[mst, Mc], writes=[dec])
        S.op("act", lambda e: e.activation(out=dec[:, :], in_=dec[:, :], func=AF.Exp), reads=[dec], writes=[dec])
        S.op("dve", lambda e: e.tensor_tensor(out=dec[:, 32:33], in0=dec[:, 32:33], in1=flag[0:4, 0:1], op=ALU.mult), reads=[dec, flag], writes=[dec])
        for half_ in range(2):
            ps = psf_rr()
            def tr(e, ps=ps, half_=half_):
                for cc_ in range(32):
                    c = half_ * 32 + cc_
                    r = e.transpose(out=ps[0:64, cc_ * 12:(cc_ + 1) * 12], in_=stk[0:12, c * 64:(c + 1) * 64], identity=identf[0:12, 0:12])
                return r
            S.op("pe", tr, reads=[stk, identf], writes=[ps])
            S.op("dve", lambda e, ps=ps, half_=half_: e.tensor_copy(out=scal[:, half_ * 32:(half_ + 1) * 32, :].rearrange("p c k -> p (c k)"), in_=ps[0:64, 0:384]),
                 reads=[ps], writes=[scal])
        sel = S.sb([4, 4, 128], F32, name="sel")
        S.op("dve", lambda e: e.tensor_copy(out=sel[:, :, :], in_=identf[0:4, 0:4].unsqueeze(2).to_broadcast([4, 4, 128])), reads=[identf], writes=[sel])
        ps = psf_rr()
        def bcm(e, ps=ps):
            for h in range(4):
                r = e.matmul(ps[:, h * 64:(h + 1) * 64], lhsT=sel[0:4, h, :], rhs=dec[0:4, :], start=True, stop=True)
            return r
        S.op("pe", bcm, reads=[sel, dec], writes=[ps])
        S.op("dve", lambda e, ps=ps: e.tensor_copy(out=decB[:, :, :].rearrange("p h c -> p (h c)"), in_=ps[:, 0:256]), reads=[ps], writes=[decB])
        if "scal" in dbg:
            dbg_out("scal", scal, [64, 64, 12], F32)
            dbg_out("decB", decB, [128, 4, 64], F32)

        S.pop()
        S.push()
        PSF, _pb = psum_std()
        psbf = _pb[0]
        psf_rr = RR(PSF)
        S.mark("2b")
        qTs = S.sb([128, 8, NTOK], BF16, name="qTs")
        kTs = S.sb([128, 8, NALL], BF16, name="kTs")
        ktok_d = S.dram("ktok_d", [NALL, 1024], BF16)
        cw = S.sb([128, 16, 4], F32, name="cw")
        cb = S.sb([128, 16], F32, name="cb")
        S.dma(cw[:, :, :], I["conv_wT"], writes=[cw])
        S.dma(cb[:, :], I["conv_b"], writes=[cb])
        raw = [S.sb([128, 2051], F32, name="raw%d" % i) for i in range(2)]
        acc = [S.sb([128, 2048], F32, name="acc%d" % i) for i in range(2)]
        it = 0
        for fc in range(16):
            isk = fc >= 8
            for tb in ([0, 2048] if isk else [2048]):
                rw = raw[it % 2]
                ac = acc[it % 2]
                it += 1
                S.dma(rw[:, 3:2051], mqkT[fc * 128:(fc + 1) * 128, tb:tb + 2048], reads=[mqkT], writes=[rw])
                if tb == 0:
                    S.op("pool", lambda e, rw=rw: e.memset(rw[:, 0:3], 0.0), writes=[rw])
                else:
                    S.dma(rw[:, 0:3], mqkT[fc * 128:(fc + 1) * 128, tb - 3:tb], reads=[mqkT], writes=[rw])
                S.op("dve", lambda e, rw=rw, ac=ac, fc=fc: e.tensor_scalar(out=ac[:, :], in0=rw[:, 0:2048], scalar1=cw[:, fc, 0:1], scalar2=None, op0=ALU.mult),
                     reads=[rw, cw], writes=[ac])
                for j in (1, 2, 3):
                    eng = "dve"
                    S.op(eng, lambda e, rw=rw, ac=ac, fc=fc, j=j: e.scalar_tensor_tensor(out=ac[:, :], in0=rw[:, j:j + 2048], scalar=cw[:, fc, j:j + 1], in1=ac[:, :], op0=ALU.mult, op1=ALU.add),
                         reads=[rw, cw, ac], writes=[ac])
                if isk:
                    dst = kTs[:, fc - 8, tb:tb + 2048]
                    dbuf = kTs
                else:
                    dst = qTs[:, fc, :]
                    dbuf = qTs
                S.op("act", lambda e, ac=ac, dst=dst, fc=fc: e.activation(out=dst, in_=ac[:, :], func=AF.Silu, bias=cb[:, fc:fc + 1]),
                     reads=[ac, cb], writes=[dbuf])
                if isk:
                    S.op("pool", lambda e, dst=dst: e.tensor_scalar(out=dst, in0=dst, scalar1=0.0625, scalar2=None, op0=ALU.mult),
                         reads=[kTs], writes=[kTs])
        ktb = [S.sb([128, 1024], BF16, name="ktb%d" % i) for i in range(2)]
        for tb in range(32):
            def trk(e, tb=tb, psbf=psbf):
                for f in range(8):
                    r = e.transpose(out=psbf[:, f * 128:(f + 1) * 128], in_=kTs[:, f, tb * 128:(tb + 1) * 128], identity=identb[:, :])
                return r
            S.op("pe", trk, reads=[kTs, identb], writes=[psbf])
            kb = ktb[tb % 2]
            cast(evac_rr(), kb[:, :], psbf[:, :], [psbf], [kb])
            S.dma(ktok_d[tb * 128:(tb + 1) * 128, :], kb[:, :], reads=[kb], writes=[ktok_d], q="act")
        if "qTs" in dbg:
            dbg_out("qTs", qTs, [128, 8, NTOK], BF16)
            dbg_out("kTs", kTs, [128, 8, NALL], BF16)
            dbg_dram("ktok", ktok_d, [NALL, 1024], BF16)

        S.mark("2c")
        cat_d = S.dram("cat_d", [NTOK, 2048], BF16)
        gB = S.sb([64, 1024], F32, name="gB")
        S.dma(gB[:, :], I["mh_g"].to_broadcast([64, 1024]), writes=[gB])
        STf = [S.sb([128, 2, 257], F32, name="STf%d" % h) for h in range(4)]
        STb = [S.sb([128, 2, 257], BF16, name="STb%d" % h) for h in range(4)]
        for h in range(4):
            S.op("pool", lambda e, h=h: e.memset(STf[h][:, :, :], 0.0), writes=[STf[h]])
        kcs = [S.sb([64, 1024], BF16, name="kc%d" % i) for i in range(3)]
        vcs = [S.sb([64, 1024], F32, name="vc%d" % i) for i in range(3)]
        mos = [S.sb([64, 1024], F32, name="moc%d" % i) for i in range(2)]
        sgs = [S.sb([64, 1024], F32, name="sg%d" % i) for i in range(2)]
        vps = [S.sb([64, 257], BF16, name="vp%d" % i) for i in range(8)]
        Sms = [S.sb([64, 64], BF16, name="Sm%d" % i) for i in range(4)]
        sm1 = [S.sb([64, 8], F32, name="sm1_%d" % i) for i in range(4)]
        hns = [S.sb([64, 256], F32, name="hn%d" % i) for i in range(4)]
        jnk = [S.sb([64, 256], F32, name="jnk%d" % i) for i in range(2)]
        catA = [S.sb([64, 1024], BF16, name="catA%d" % i) for i in range(2)]
        flat = lambda b_: b_[:, :, :].rearrange("p a b -> p (a b)")
        for c in range(64):
            own = c >= 32
            kc = kcs[c % 3]
            vc = vcs[c % 3]
            S.dma(kc[:, :], ktok_d[c * 64:(c + 1) * 64, :], reads=[ktok_d], writes=[kc])
            S.dma(vc[:, :], mv_d[c * 64:(c + 1) * 64, :], reads=[mv_d], writes=[vc])
            if own:
                mo_ = mos[c % 2]
                sg = sgs[c % 2]
                ca = catA[c % 2]
                S.dma(mo_[:, :], mo_d[c * 64:(c + 1) * 64, :], reads=[mo_d], writes=[mo_])
                S.op("act", lambda e, mo_=mo_, sg=sg: e.activation(out=sg[:, :], in_=mo_[:, :], func=AF.Sigmoid), reads=[mo_], writes=[sg])
            for h in range(4):
                vp = vps[(c * 4 + h) % 8]
                p_ap = scal[:, c, h:h + 1]
                r_ap = scal[:, c, 4 + h:5 + h]
                f_ap = scal[:, c, 8 + h:9 + h]
                S.op("dve", lambda e, vp=vp, vc=vc, h=h, p_ap=p_ap: e.tensor_scalar(out=vp[:, 0:256], in0=vc[:, h * 256:(h + 1) * 256], scalar1=p_ap, scalar2=None, op0=ALU.mult),
                     reads=[vc, scal], writes=[vp])
                S.op("pool", lambda e, vp=vp, p_ap=p_ap: e.tensor_copy(out=vp[:, 256:257], in_=p_ap), reads=[scal, vp], writes=[vp])
                S.op("dve", lambda e, h=h, c=c: e.tensor_scalar(out=flat(STf[h]), in0=flat(STf[h]), scalar1=decB[:, h, c:c + 1], scalar2=None, op0=ALU.mult),
                     reads=[STf[h], decB], writes=[STf[h]])
                if own:
                    S.op("act", lambda e, h=h: e.activation(out=flat(STb[h]), in_=flat(STf[h]), func=AF.Copy), reads=[STf[h]], writes=[STb[h]])
                    co = c - 32
                    psS = psf_rr()
                    def mmS(e, psS=psS, h=h, c=c, co=co):
                        for dc in range(2):
                            r = e.matmul(psS[0:64, 0:64], lhsT=kTs[:, h * 2 + dc, c * 64:(c + 1) * 64], rhs=qTs[:, h * 2 + dc, co * 64:(co + 1) * 64],
                                         start=(dc == 0), stop=(dc == 1))
                        return r
                    S.op("pe", mmS, reads=[kTs, qTs], writes=[psS])
                    Sm = Sms[h]
                    S.op("dve", lambda e, psS=psS, Sm=Sm: e.tensor_tensor(out=Sm[:, :], in0=psS[0:64, 0:64], in1=cmask[:, :], op=ALU.mult),
                         reads=[psS, cmask], writes=[Sm])
                    psO = psf_rr()
                    def mmO(e, psO=psO, Sm=Sm, vp=vp, h=h, co=co):
                        e.matmul(psO[0:64, 0:257], lhsT=Sm[:, :], rhs=vp[:, :], start=True, stop=False)
                        for dc in range(2):
                            r = e.matmul(psO[0:64, 0:257], lhsT=qTs[:, h * 2 + dc, co * 64:(co + 1) * 64], rhs=STb[h][:, dc, :],
                                         start=False, stop=(dc == 1))
                        return r
                    S.op("pe", mmO, reads=[Sm, vp, qTs, STb[h]], writes=[psO])
                psU = [psf_rr(), psf_rr()]
                for dc in range(2):
                    S.op("pe", lambda e, dc=dc, psU=psU, kc=kc, vp=vp, h=h: e.matmul(psU[dc][:, 0:257], lhsT=kc[:, h * 256 + dc * 128:h * 256 + (dc + 1) * 128], rhs=vp[:, :], start=True, stop=True),
                         reads=[kc, vp], writes=[psU[dc]])
                for dc in range(2):
                    eng = "dve" if dc == 0 else "pool"
                    if eng == "pool":
                        eng = "dve"
                    S.op(eng, lambda e, dc=dc, psU=psU, h=h: e.tensor_tensor(out=STf[h][:, dc, :], in0=STf[h][:, dc, :], in1=psU[dc][:, 0:257], op=ALU.add),
                         reads=[STf[h], psU[dc]], writes=[STf[h]])
                if own:
                    s1 = sm1[h]
                    hn = hns[h]
                    jk = jnk[h % 2]
                    S.op("dve", lambda e, s1=s1, psO=psO, r_ap=r_ap: e.tensor_tensor(out=s1[:, 0:1], in0=psO[0:64, 256:257], in1=r_ap, op=ALU.mult),
                         reads=[psO, scal], writes=[s1])
                    S.op("dve", lambda e, s1=s1: e.tensor_scalar(out=s1[:, 7:8], in0=s1[:, 0:1], scalar1=-1.0, scalar2=None, op0=ALU.mult),
                         reads=[s1], writes=[s1])
                    S.op("dve", lambda e, s1=s1: e.tensor_tensor(out=s1[:, 7:8], in0=s1[:, 7:8], in1=s1[:, 0:1], op=ALU.max),
                         reads=[s1], writes=[s1])
                    S.op("dve", lambda e, s1=s1, f_ap=f_ap: e.tensor_tensor(out=s1[:, 1:2], in0=s1[:, 7:8], in1=f_ap, op=ALU.max),
                         reads=[s1, scal], writes=[s1])
                    S.op("dve", lambda e, s1=s1: e.reciprocal(out=s1[:, 2:3], in_=s1[:, 1:2]), reads=[s1], writes=[s1])
                    S.op("dve", lambda e, s1=s1, r_ap=r_ap: e.tensor_tensor(out=s1[:, 3:4], in0=s1[:, 2:3], in1=r_ap, op=ALU.mult),
                         reads=[s1, scal], writes=[s1])
                    S.op("act", lambda e, hn=hn, psO=psO, s1=s1: e.activation(out=hn[:, :], in_=psO[0:64, 0:256], func=AF.Copy, scale=s1[:, 3:4]),
                         reads=[psO, s1], writes=[hn])
                    S.op("dve", lambda e, hn=hn, s1=s1: e.reduce_sum(out=s1[:, 4:5], in_=hn[:, :], axis=AX.X), reads=[hn, s1], writes=[s1])
                    S.op("dve", lambda e, s1=s1: e.tensor_scalar(out=s1[:, 4:5], in0=s1[:, 4:5], scalar1=-1.0 / 256.0, scalar2=None, op0=ALU.mult),
                         reads=[s1], writes=[s1])
                    S.op("act", lambda e, hn=hn, jk=jk, s1=s1: e.activation(out=jk[:, :], in_=hn[:, :], func=AF.Square, bias=s1[:, 4:5]),
                         reads=[hn, s1], writes=[jk])
                    S.op("dve", lambda e, jk=jk, s1=s1: e.reduce_sum(out=s1[:, 5:6], in_=jk[:, :], axis=AX.X), reads=[jk, s1], writes=[s1])
                    S.op("dve", lambda e, s1=s1: e.tensor_scalar(out=s1[:, 5:6], in0=s1[:, 5:6], scalar1=1.0 / 256.0, scalar2=LN_EPS, op0=ALU.mult, op1=ALU.add),
                         reads=[s1], writes=[s1])
                    S.op("act", lambda e, s1=s1: e.activation(out=s1[:, 6:7], in_=s1[:, 5:6], func=AF.Ln), reads=[s1], writes=[s1])
                    S.op("act", lambda e, s1=s1: e.activation(out=s1[:, 5:6], in_=s1[:, 6:7], func=AF.Exp, scale=-0.5), reads=[s1], writes=[s1])
                    S.op("dve", lambda e, hn=hn, s1=s1: e.tensor_scalar(out=hn[:, :], in0=hn[:, :], scalar1=s1[:, 4:5], scalar2=s1[:, 5:6], op0=ALU.add, op1=ALU.mult),
                         reads=[hn, s1], writes=[hn])
                    S.op("pool", lambda e, hn=hn, h=h: e.tensor_tensor(out=hn[:, :], in0=hn[:, :], in1=gB[:, h * 256:(h + 1) * 256], op=ALU.mult),
                         reads=[hn, gB], writes=[hn])
                    S.op("pool", lambda e, hn=hn, h=h, sg=sg, ca=ca: e.tensor_tensor(out=ca[:, h * 256:(h + 1) * 256], in0=hn[:, :], in1=sg[:, h * 256:(h + 1) * 256], op=ALU.mult),
                         reads=[hn, sg], writes=[ca])
            if own:
                S.dma(cat_d[(c - 32) * 64:(c - 31) * 64, 0:1024], ca[:, :], reads=[ca], writes=[cat_d], q="act")
        dbg_dram("cat", cat_d, [NTOK, 2048], BF16)
        if stop_after <= 2:
            return early_exit()
        S.pop()

        S.mark("3")
        S.push()
        PSF, _pb = psum_std()
        psbf = _pb[0]
        psf_rr = RR(PSF)
        SCALE = 128.0 ** -0.5
        anum_d = [S.dram("anum%d" % p, [NTOK, 1024], F32) for p in range(3)]
        am_d = [S.dram("am%d" % p, [NTOK, 8], F32) for p in range(3)]
        ad_d = [S.dram("ad%d" % p, [NTOK, 8], F32) for p in range(3)]
        qTa = S.sb([128, 8, NTOK], BF16, name="qTa")
        kTa = S.sb([128, 8, NALL], BF16, name="kTa")
        maskA = S.sb([128, 256], F32, name="maskA")
        maskC = S.sb([128, 256], F32, name="maskC")
        S.dma(maskA[:, :], I["maskA"], writes=[maskA])
        S.dma(maskC[:, :], I["maskC"], writes=[maskC])
        xts = [S.sb([128, 1024], F32, name="rx%d" % i) for i in range(2)]
        xbs = [S.sb([128, 1024], BF16, name="rxb%d" % i) for i in range(2)]
        csts = [S.sb([128, 32], F32, name="cst%d" % i) for i in range(2)]
        rt = [S.sb([128, 8, 16], F32, name="rt%d" % i) for i in range(4)]
        it = 0
        for (src, dstT, t_lo) in ((ak_d, kTa, 0), (aq_d, qTa, 16)):
            for tt in range(t_lo, 32):
                xt = xts[it % 2]
                xbt = xbs[it % 2]
                cst = csts[it % 2]
                it += 1
                S.dma(xt[:, :], src[tt * 128:(tt + 1) * 128, :], reads=[src], writes=[xt])
                S.dma(cst[:, :], I["cs"][tt * 128:(tt + 1) * 128, :], writes=[cst])
                cast("act", xbt[:, :], xt[:, :], [xt], [xbt])
                x3 = xt[:, :].rearrange("p (h d) -> p h d", d=128)
                xb3 = xbt[:, :].rearrange("p (h d) -> p h d", d=128)
                cosb = cst[:, 0:16].unsqueeze(1).to_broadcast([128, 8, 16])
                sinb = cst[:, 16:32].unsqueeze(1).to_broadcast([128, 8, 16])
                S.op("dve", lambda e, x3=x3, cosb=cosb: e.tensor_tensor(out=rt[0][:, :, :], in0=x3[:, :, 0:16], in1=cosb, op=ALU.mult), reads=[xt, cst], writes=[rt[0]])
                S.op("pool", lambda e, x3=x3, sinb=sinb: e.tensor_tensor(out=rt[1][:, :, :], in0=x3[:, :, 16:32], in1=sinb, op=ALU.mult), reads=[xt, cst], writes=[rt[1]])
                S.op("dve", lambda e, x3=x3, sinb=sinb: e.tensor_tensor(out=rt[2][:, :, :], in0=x3[:, :, 0:16], in1=sinb, op=ALU.mult), reads=[xt, cst], writes=[rt[2]])
                S.op("pool", lambda e, x3=x3, cosb=cosb: e.tensor_tensor(out=rt[3][:, :, :], in0=x3[:, :, 16:32], in1=cosb, op=ALU.mult), reads=[xt, cst], writes=[rt[3]])
                S.op("dve", lambda e, xb3=xb3: e.tensor_tensor(out=xb3[:, :, 0:16], in0=rt[0][:, :, :], in1=rt[1][:, :, :], op=ALU.subtract), reads=[rt[0], rt[1], xbt], writes=[xbt])
                S.op("dve", lambda e, xb3=xb3: e.tensor_tensor(out=xb3[:, :, 16:32], in0=rt[2][:, :, :], in1=rt[3][:, :, :], op=ALU.add), reads=[rt[2], rt[3], xbt], writes=[xbt])
                def trr(e, xbt=xbt, psbf=psbf):
                    for h in range(8):
                        r = e.transpose(out=psbf[:, h * 128:(h + 1) * 128], in_=xbt[:, h * 128:(h + 1) * 128], identity=identb[:, :])
                    return r
                S.op("pe", trr, reads=[xbt, identb], writes=[psbf])
                tl = tt - t_lo
                cast(evac_rr(), dstT[:, :, tl * 128:(tl + 1) * 128], psbf[:, :].rearrange("p (h t) -> p h t", t=128), [psbf], [dstT])
        vts = [S.sb([128, 2, 1024], BF16, name="vt%d" % i) for i in range(2)]
        sms = [S.sb([128, 256], F32, name="sm%d" % i) for i in range(2)]
        Pbs = [S.sb([128, 256], BF16, name="Pb%d" % i) for i in range(2)]
        PTs = [S.sb([128, 2, 128], BF16, name="PT%d" % i) for i in range(2)]
        mxs = [S.sb([128, 2], F32, name="mx%d" % i) for i in range(2)]
        resN = [S.sb([128, 8, 128], F32, name="resN%d" % i) for i in range(2)]
        resM = [S.sb([128, 8], F32, name="resM%d" % i) for i in range(2)]
        resD = [S.sb([128, 8], F32, name="resD%d" % i) for i in range(2)]
        u = 0
        for pi, (W_, Dl) in enumerate(((128, 1), (512, 4), (2048, 16))):
            nblk = 16 // Dl
            av_v = av_d[:, :].rearrange("(l d) c -> d l c", d=Dl)
            for r_ in range(Dl):
                for n_ in range(nblk):
                    nbg = nblk + n_
                    vt = vts[u % 2]
                    rN, rM, rD = resN[u % 2], resM[u % 2], resD[u % 2]
                    u += 1
                    S.dma(vt[:, :, :], av_v[r_, (nbg - 1) * 128:(nbg + 1) * 128, :].rearrange("(b p) c -> p b c", p=128), reads=[av_d], writes=[vt])
                    mk_ = maskC if n_ == 0 else maskA
                    for h in range(8):
                        q_ap = qTa[:, h, :].rearrange("p (l d) -> p d l", d=Dl)[:, r_, n_ * 128:(n_ + 1) * 128]
                        k_ap = kTa[:, h, :].rearrange("p (l d) -> p d l", d=Dl)[:, r_, (nbg - 1) * 128:(nbg + 1) * 128]
                        psS = psf_rr()
                        S.op("pe", lambda e, psS=psS, q_ap=q_ap, k_ap=k_ap: e.matmul(psS[:, 0:256], lhsT=q_ap, rhs=k_ap, start=True, stop=True), reads=[qTa, kTa], writes=[psS])
                        sm = sms[h % 2]
                        Pb = Pbs[h % 2]
                        PT = PTs[h % 2]
                        mx = mxs[h % 2]
                        S.op("dve", lambda e, sm=sm, psS=psS, mk_=mk_: e.scalar_tensor_tensor(out=sm[:, :], in0=psS[:, 0:256], scalar=SCALE, in1=mk_[:, :], op0=ALU.mult, op1=ALU.add),
                             reads=[psS, mk_], writes=[sm])
                        S.op("dve", lambda e, sm=sm, rM=rM, h=h: e.reduce_max(out=rM[:, h:h + 1], in_=sm[:, :], axis=AX.X), reads=[sm, rM], writes=[rM])
                        S.op("dve", lambda e, mx=mx, rM=rM, h=h: e.tensor_scalar(out=mx[:, 0:1], in0=rM[:, h:h + 1], scalar1=-1.0, scalar2=None, op0=ALU.mult), reads=[rM], writes=[mx])
                        S.op("act", lambda e, Pb=Pb, sm=sm, mx=mx: e.activation(out=Pb[:, :], in_=sm[:, :], func=AF.Exp, bias=mx[:, 0:1]), reads=[sm, mx], writes=[Pb])
                        S.op("dve", lambda e, Pb=Pb, rD=rD, h=h: e.reduce_sum(out=rD[:, h:h + 1], in_=Pb[:, :], axis=AX.X), reads=[Pb, rD], writes=[rD])
                        def trp(e, Pb=Pb, psbf=psbf):
                            e.transpose(out=psbf[:, 0:128], in_=Pb[:, 0:128], identity=identb[:, :])
                            return e.transpose(out=psbf[:, 128:256], in_=Pb[:, 128:256], identity=identb[:, :])
                        S.op("pe", trp, reads=[Pb, identb], writes=[psbf])
                        cast("act", PT[:, :, :].rearrange("p b q -> p (b q)"), psbf[:, 0:256], [psbf], [PT])
                        psO = psf_rr()
                        def mmo(e, psO=psO, PT=PT, vt=vt, h=h):
                            e.matmul(psO[:, 0:128], lhsT=PT[:, 0, :], rhs=vt[:, 0, h * 128:(h + 1) * 128], start=True, stop=False)
                            return e.matmul(psO[:, 0:128], lhsT=PT[:, 1, :], rhs=vt[:, 1, h * 128:(h + 1) * 128], start=False, stop=True)
                        S.op("pe", mmo, reads=[PT, vt], writes=[psO])
                        S.op("pool" if False else "dve", lambda e, rN=rN, psO=psO, h=h: e.tensor_copy(out=rN[:, h, :], in_=psO[:, 0:128]), reads=[psO, rN], writes=[rN])
                    rows = lambda dd: dd[:, :].rearrange("(l d) c -> d l c", d=Dl)[r_, n_ * 128:(n_ + 1) * 128, :]
                    S.dma(rows(anum_d[pi]), rN[:, :, :].rearrange("p h d -> p (h d)"), reads=[rN], writes=[anum_d[pi]], q="act")
                    S.dma(rows(am_d[pi]), rM[:, :], reads=[rM], writes=[am_d[pi]], q="act")
                    S.dma(rows(ad_d[pi]), rD[:, :], reads=[rD], writes=[ad_d[pi]], q="act")
        S.mark("3merge")
        n3s = [S.sb([128, 3, 1024], F32, name="n3_%d" % i) for i in range(2)]
        m3s = [S.sb([128, 3, 8], F32, name="m3_%d" % i) for i in range(2)]
        d3s = [S.sb([128, 3, 8], F32, name="d3_%d" % i) for i in range(2)]
        w3s = [S.sb([128, 3, 8], F32, name="w3_%d" % i) for i in range(2)]
        mMs = [S.sb([128, 16], F32, name="mM_%d" % i) for i in range(2)]
        obs = [S.sb([128, 1024], BF16, name="ob_%d" % i) for i in range(2)]
        for tt in range(16):
            n3, m3, d3, w3_, mM, ob = n3s[tt % 2], m3s[tt % 2], d3s[tt % 2], w3s[tt % 2], mMs[tt % 2], obs[tt % 2]
            for p in range(3):
                S.dma(n3[:, p, :], anum_d[p][tt * 128:(tt + 1) * 128, :], reads=[anum_d[p]], writes=[n3])
                S.dma(m3[:, p, :], am_d[p][tt * 128:(tt + 1) * 128, :], reads=[am_d[p]], writes=[m3])
                S.dma(d3[:, p, :], ad_d[p][tt * 128:(tt + 1) * 128, :], reads=[ad_d[p]], writes=[d3])
            S.op("dve", lambda e, mM=mM, m3=m3: e.tensor_tensor(out=mM[:, 0:8], in0=m3[:, 0, :], in1=m3[:, 1, :], op=ALU.max), reads=[m3], writes=[mM])
            S.op("dve", lambda e, mM=mM, m3=m3: e.tensor_tensor(out=mM[:, 0:8], in0=mM[:, 0:8], in1=m3[:, 2, :], op=ALU.max), reads=[m3, mM], writes=[mM])
            S.op("dve", lambda e, mM=mM, m3=m3, w3_=w3_: e.tensor_tensor(out=w3_[:, :, :], in0=m3[:, :, :], in1=mM[:, 0:8].unsqueeze(1).to_broadcast([128, 3, 8]), op=ALU.subtract), reads=[m3, mM], writes=[w3_])
            S.op("act", lambda e, w3_=w3_: e.activation(out=w3_[:, :, :], in_=w3_[:, :, :], func=AF.Exp), reads=[w3_], writes=[w3_])
            S.op("dve", lambda e, d3=d3, w3_=w3_: e.tensor_tensor(out=d3[:, :, :], in0=d3[:, :, :], in1=w3_[:, :, :], op=ALU.mult), reads=[d3, w3_], writes=[d3])
            S.op("dve", lambda e, d3=d3, mM=mM: e.tensor_tensor(out=mM[:, 8:16], in0=d3[:, 0, :], in1=d3[:, 1, :], op=ALU.add), reads=[d3], writes=[mM])
            S.op("dve", lambda e, d3=d3, mM=mM: e.tensor_tensor(out=mM[:, 8:16], in0=mM[:, 8:16], in1=d3[:, 2, :], op=ALU.add), reads=[d3, mM], writes=[mM])
            S.op("dve", lambda e, mM=mM: e.reciprocal(out=mM[:, 8:16], in_=mM[:, 8:16]), reads=[mM], writes=[mM])
            S.op("dve", lambda e, mM=mM, w3_=w3_: e.tensor_tensor(out=w3_[:, :, :], in0=w3_[:, :, :], in1=mM[:, 8:16].unsqueeze(1).to_broadcast([128, 3, 8]), op=ALU.mult), reads=[mM, w3_], writes=[w3_])
            for p in range(3):
                eng = ("dve", "pool", "dve")[p]
                S.op(eng, lambda e, n3=n3, w3_=w3_, p=p: e.tensor_tensor(out=n3[:, p, :].rearrange("p (h d) -> p h d", d=128), in0=n3[:, p, :].rearrange("p (h d) -> p h d", d=128),
                                                                  in1=w3_[:, p, :].unsqueeze(2).to_broadcast([128, 8, 128]), op=ALU.mult), reads=[n3, w3_], writes=[n3])
            S.op("pool", lambda e, n3=n3: e.tensor_tensor(out=n3[:, 0, :], in0=n3[:, 0, :], in1=n3[:, 1, :], op=ALU.add), reads=[n3], writes=[n3])
            S.op("dve", lambda e, n3=n3, ob=ob: e.tensor_tensor(out=ob[:, :], in0=n3[:, 0, :], in1=n3[:, 2, :], op=ALU.add), reads=[n3], writes=[ob])
            S.dma(cat_d[tt * 128:(tt + 1) * 128, 1024:2048], ob[:, :], reads=[ob], writes=[cat_d], q="act")
        dbg_dram("cat3", cat_d, [NTOK, 2048], BF16)
        if stop_after <= 3:
            return early_exit()
        S.pop()

        def layer_norm_tile(y, gBt, bBt, outt, st8, jk):
            S.op("dve", lambda e: e.reduce_sum(out=st8[:, 0:1], in_=y[:, :], axis=AX.X), reads=[y, st8], writes=[st8])
            S.op("dve", lambda e: e.tensor_scalar(out=st8[:, 0:1], in0=st8[:, 0:1], scalar1=-1.0 / 2048.0, scalar2=None, op0=ALU.mult), reads=[st8], writes=[st8])
            S.op("act", lambda e: e.activation(out=jk[:, :], in_=y[:, :], func=AF.Square, bias=st8[:, 0:1]), reads=[y, st8], writes=[jk])
            S.op("dve", lambda e: e.reduce_sum(out=st8[:, 1:2], in_=jk[:, :], axis=AX.X), reads=[jk, st8], writes=[st8])
            S.op("dve", lambda e: e.tensor_scalar(out=st8[:, 1:2], in0=st8[:, 1:2], scalar1=1.0 / 2048.0, scalar2=LN_EPS, op0=ALU.mult, op1=ALU.add), reads=[st8], writes=[st8])
            S.op("act", lambda e: e.activation(out=st8[:, 2:3], in_=st8[:, 1:2], func=AF.Ln), reads=[st8], writes=[st8])
            S.op("act", lambda e: e.activation(out=st8[:, 1:2], in_=st8[:, 2:3], func=AF.Exp, scale=-0.5), reads=[st8], writes=[st8])
            S.op("dve", lambda e: e.tensor_scalar(out=y[:, :], in0=y[:, :], scalar1=st8[:, 0:1], scalar2=st8[:, 1:2], op0=ALU.add, op1=ALU.mult), reads=[y, st8], writes=[y])
            S.op("pool", lambda e: e.tensor_tensor(out=y[:, :], in0=y[:, :], in1=gBt[:, :], op=ALU.mult), reads=[y, gBt], writes=[y])
            S.op("pool", lambda e: e.tensor_tensor(out=outt[:, :], in0=y[:, :], in1=bBt[:, :], op=ALU.add), reads=[y, bBt], writes=[outt])

        def transpose16(src_bf, dstT, psbf):
            for rnd in range(2):
                def trx(e, rnd=rnd, psbf=psbf):
                    for j in range(8):
                        k = rnd * 8 + j
                        r = e.transpose(out=psbf[:, j * 128:(j + 1) * 128], in_=src_bf[:, k * 128:(k + 1) * 128], identity=identb[:, :])
                    return r
                S.op("pe", trx, reads=[src_bf, identb], writes=[psbf])
                cast(evac_rr(), dstT[:, rnd * 8:(rnd + 1) * 8, :].rearrange("p k t -> p (k t)"), psbf[:, :], [psbf], [dstT])

        S.mark("4")
        S.push()
        PSF, _pb = psum_std()
        psbf = _pb[0]
        psf_rr = RR(PSF)
        x1_d = S.dram("x1_d", [NTOK, D], F32)
        x1b_d = S.dram("x1b_d", [NTOK, D], BF16)
        WB = S.sb([128, 16, 2048], BF16, name="WB")
        S.dma(WB[:, :, :], w_out_b[:, :].rearrange("(k p) c -> p k c", p=128), reads=[w_out_b], writes=[WB])
        gB1 = S.sb([128, 2048], F32, name="gB1")
        bB1 = S.sb([128, 2048], F32, name="bB1")
        S.dma(gB1[:, :], I["ln1_g"].to_broadcast([128, 2048]), writes=[gB1])
        S.dma(bB1[:, :], I["ln1_b"].to_broadcast([128, 2048]), writes=[bB1])
        cts = [S.sb([128, 2048], BF16, name="ct%d" % i) for i in range(2)]
        cTs = [S.sb([128, 16, 128], BF16, name="cT%d" % i) for i in range(2)]
        xos = [S.sb([128, 2048], F32, name="xo%d" % i) for i in range(2)]
        ys = [S.sb([128, 2048], F32, name="y%d" % i) for i in range(2)]
        jks = [S.sb([128, 2048], F32, name="jk%d" % i) for i in range(1)]
        st8s = [S.sb([128, 8], F32, name="st8_%d" % i) for i in range(2)]
        x1bs = [S.sb([128, 2048], BF16, name="x1b%d" % i) for i in range(2)]
        for tt in range(16):
            ct, cT, xo, y, st8, x1b = cts[tt % 2], cTs[tt % 2], xos[tt % 2], ys[tt % 2], st8s[tt % 2], x1bs[tt % 2]
            S.dma(ct[:, :], cat_d[tt * 128:(tt + 1) * 128, :], reads=[cat_d], writes=[ct])
            S.dma(xo[:, :], I["xown"][tt * 128:(tt + 1) * 128, :], writes=[xo])
            transpose16(ct, cT, psbf)
            for cc in range(4):
                ps = psf_rr()
                def mmw(e, ps=ps, cT=cT, cc=cc, WB=WB):
                    for k in range(16):
                        r = e.matmul(ps[:, :], lhsT=cT[:, k, :], rhs=WB[:, k, cc * 512:(cc + 1) * 512], start=(k == 0), stop=(k == 15))
                    return r
                S.op("pe", mmw, reads=[cT, WB], writes=[ps])
                S.op("dve", lambda e, ps=ps, y=y, xo=xo, cc=cc: e.scalar_tensor_tensor(out=y[:, cc * 512:(cc + 1) * 512], in0=xo[:, cc * 512:(cc + 1) * 512], scalar=ALPHA, in1=ps[:, :], op0=ALU.mult, op1=ALU.add),
                     reads=[ps, xo, y], writes=[y])
            layer_norm_tile(y, gB1, bB1, xo, st8, jks[0])
            S.dma(x1_d[tt * 128:(tt + 1) * 128, :], xo[:, :], reads=[xo], writes=[x1_d], q="act")
            cast("act", x1b[:, :], xo[:, :], [xo], [x1b])
            S.dma(x1b_d[tt * 128:(tt + 1) * 128, :], x1b[:, :], reads=[x1b], writes=[x1b_d], q="act")
        dbg_dram("x1", x1_d, [NTOK, D], F32)
        if stop_after <= 4:
            return early_exit()
        S.pop()

        S.mark("5A")
        S.push()
        P2 = [S.ps([128, 1024], F32) for i in range(3)]
        cvf = [S.sb([128, 2048], F32, name="cvf%d" % i) for i in range(2)]
        cvb = [S.sb([128, 2048], BF16, name="cvb%d" % i) for i in range(2)]
        cv_steps = []
        for (src_ap, dst, R, C) in ((I["uT"], uT_b, D, 16384), (I["ev"], ev_b, 16384, D)):
            for r0 in range(0, R, 128):
                for c0 in range(0, C, 2048):
                    cv_steps.append((src_ap, dst, r0, c0))
        cv_i = 0
        psq = S.ps([128, 512], F32)
        psbf = Buf(psq[:, :].bitcast(BF16))
        psbf.writers = psq.writers
        psbf.readers = psq.readers
        Gd = S.dram("Gd", [NTOK, 16384], BF16)
        x1T_d = S.dram("x1T_d", [D, NTOK], BF16)
        WQ = S.sb([128, 16, 2048], BF16, name="WQ")
        S.dma(WQ[:, :, :], w_q_b[:, :].rearrange("(k p) c -> p k c", p=128), reads=[w_q_b], writes=[WQ])
        kf = S.sb([128, 2, 128], F32, name="kf")
        kb2 = S.sb([128, 2, 128], BF16, name="kb2")
        S.dma(kf[:, 0, :], I["k1T"], writes=[kf])
        S.dma(kf[:, 1, :], I["k2T"], writes=[kf])
        S.op("dve", lambda e: e.tensor_copy(out=kb2[:, :, :], in_=kf[:, :, :]), reads=[kf], writes=[kb2])
        PK1T = S.sb([128, 128, 128], BF16, name="PK1T")
        S.op("dve", lambda e: e.tensor_copy(out=PK1T[:, :, :], in_=kb2[:, 0, :].unsqueeze(2).to_broadcast([128, 128, 128])), reads=[kb2], writes=[PK1T])
        K2rep = S.sb([128, 4, 128], BF16, name="K2rep")
        S.op("dve", lambda e: e.tensor_copy(out=K2rep[:, :, :], in_=kb2[:, 1, :].unsqueeze(1).to_broadcast([128, 4, 128])), reads=[kb2], writes=[K2rep])
        PKf = PK1T[:, :, :].rearrange("p a b -> p (a b)")
        K2f = K2rep[:, :, :].rearrange("p a b -> p (a b)")
        xb1s = [S.sb([128, 2048], BF16, name="xb1_%d" % i) for i in range(2)]
        x1Ts = [S.sb([128, 16, 128], BF16, name="x1T_%d" % i) for i in range(2)]
        qTts = [S.sb([128, 16, 128], BF16, name="qTt_%d" % i) for i in range(2)]
        scs = [S.sb([128, 16, 128], F32, name="sc_%d" % i) for i in range(2)]
        tmpk = S.sb([128, 256], F32, name="tmpk")
        m16s = [S.sb([128, 16, 16], F32, name="m16_%d" % i) for i in range(2)]
        cand = S.sb([128, 8, 256], F32, name="cand")
        t16s = [S.sb([128, 8, 16], F32, name="t16_%d" % i) for i in range(2)]
        e16 = S.sb([128, 8, 16], F32, name="e16")
        tzs = [S.sb([128, 32], F32, name="tz_%d" % i) for i in range(2)]
        Es = [S.sb([128, 1024], BF16, name="E_%d" % i) for i in range(3)]
        Gms = [S.sb([128, 1024], BF16, name="Gm_%d" % i) for i in range(3)]
        Gts = [S.sb([128, 1024], BF16, name="Gt_%d" % i) for i in range(2)]
        BIGNEG = -1.0e30
        gi = 0
        for tt in range(16):
            xb1, x1T, qTt, sc = xb1s[tt % 2], x1Ts[tt % 2], qTts[tt % 2], scs[tt % 2]
            m16, t16, tz = m16s[tt % 2], t16s[tt % 2], tzs[tt % 2]
            S.dma(xb1[:, :], x1b_d[tt * 128:(tt + 1) * 128, :], reads=[x1b_d], writes=[xb1])
            transpose16(xb1, x1T, psbf)
            S.dma(x1T_d[:, :].rearrange("(k p) t -> p k t", p=128)[:, :, tt * 128:(tt + 1) * 128], x1T[:, :, :], reads=[x1T], writes=[x1T_d], q="act")
            for q4 in range(4):
                def mmq(e, q4=q4, x1T=x1T, psq=psq, WQ=WQ):
                    for j in range(4):
                        qc = q4 * 4 + j
                        for k in range(16):
                            r = e.matmul(psq[:, j * 128:(j + 1) * 128], lhsT=WQ[:, k, qc * 128:(qc + 1) * 128], rhs=x1T[:, k, :], start=(k == 0), stop=(k == 15))
                    return r
                S.op("pe", mmq, reads=[WQ, x1T], writes=[psq])
                cast(evac_rr(), qTt[:, q4 * 4:(q4 + 1) * 4, :].rearrange("p a b -> p (a b)"), psq[:, :], [psq], [qTt])
            for q4 in range(4):
                def mms(e, q4=q4, qTt=qTt, psq=psq, kb2=kb2):
                    for j in range(4):
                        qc = q4 * 4 + j
                        r = e.matmul(psq[:, j * 128:(j + 1) * 128], lhsT=qTt[:, qc, :], rhs=kb2[:, qc % 2, :], start=True, stop=True)
                    return r
                S.op("pe", mms, reads=[qTt, kb2], writes=[psq])
                cast(evac_rr(), sc[:, q4 * 4:(q4 + 1) * 4, :].rearrange("p a b -> p (a b)"), psq[:, :], [psq], [sc])
            for qc in range(16):
                S.op("dve", lambda e, qc=qc, sc=sc, m16=m16: e.max(out=m16[:, qc, 0:8], in_=sc[:, qc, :]), reads=[sc, m16], writes=[m16])
                S.op("dve", lambda e, qc=qc, sc=sc, m16=m16: e.match_replace(out=tmpk[:, 0:128], in_to_replace=m16[:, qc, 0:8], in_values=sc[:, qc, :], imm_value=BIGNEG), reads=[sc, m16], writes=[tmpk])
                S.op("dve", lambda e, qc=qc, m16=m16: e.max(out=m16[:, qc, 8:16], in_=tmpk[:, 0:128]), reads=[tmpk, m16], writes=[m16])
            m16v = m16[:, :, :].rearrange("p (h two) k -> p h two k", two=2)
            S.op("dve", lambda e, m16v=m16v: e.tensor_tensor(out=cand[:, :, :].rearrange("p h (i j) -> p h i j", j=16), in0=m16v[:, :, 0, :].unsqueeze(3).to_broadcast([128, 8, 16, 16]),
                                                            in1=m16v[:, :, 1, :].unsqueeze(2).to_broadcast([128, 8, 16, 16]), op=ALU.add), reads=[m16], writes=[cand])
            for h in range(8):
                S.op("dve", lambda e, h=h, t16=t16: e.max(out=t16[:, h, 0:8], in_=cand[:, h, :]), reads=[cand, t16], writes=[t16])
                S.op("dve", lambda e, h=h, t16=t16: e.match_replace(out=tmpk[:, :], in_to_replace=t16[:, h, 0:8], in_values=cand[:, h, :], imm_value=BIGNEG), reads=[cand, t16], writes=[tmpk])
                S.op("dve", lambda e, h=h, t16=t16: e.max(out=t16[:, h, 8:16], in_=tmpk[:, :]), reads=[tmpk, t16], writes=[t16])
            S.op("dve", lambda e, t16=t16, tz=tz: e.tensor_scalar(out=tz[:, 0:8], in0=t16[:, :, 15], scalar1=-2.0e-5, scalar2=None, op0=ALU.add), reads=[t16, tz], writes=[tz])
            S.op("dve", lambda e, t16=t16: e.tensor_tensor(out=e16[:, :, :], in0=t16[:, :, :], in1=t16[:, :, 0:1].to_broadcast([128, 8, 16]), op=ALU.subtract), reads=[t16], writes=[e16])
            S.op("act", lambda e: e.activation(out=e16[:, :, :], in_=e16[:, :, :], func=AF.Exp), reads=[e16], writes=[e16])
            S.op("dve", lambda e, tz=tz: e.reduce_sum(out=tz[:, 8:16], in_=e16[:, :, :], axis=AX.X), reads=[e16, tz], writes=[tz])
            S.op("act", lambda e, tz=tz: e.activation(out=tz[:, 8:16], in_=tz[:, 8:16], func=AF.Ln), reads=[tz], writes=[tz])
            S.op("dve", lambda e, tz=tz, t16=t16: e.tensor_tensor(out=tz[:, 16:24], in0=tz[:, 8:16], in1=t16[:, :, 0], op=ALU.add), reads=[tz, t16], writes=[tz])
            S.op("dve", lambda e, tz=tz: e.tensor_scalar(out=tz[:, 16:24], in0=tz[:, 16:24], scalar1=-1.0, scalar2=None, op0=ALU.mult), reads=[tz], writes=[tz])
            for eg in range(16):
                Gt = Gts[eg % 2]
                if cv_i < len(cv_steps):
                    src_ap, dst, r0, c0 = cv_steps[cv_i]
                    f_, b_ = cvf[cv_i % 2], cvb[cv_i % 2]
                    cv_i += 1
                    S.dma(f_[:, :], src_ap[r0:r0 + 128, c0:c0 + 2048], writes=[f_])
                    S.op("pool", lambda e, f_=f_, b_=b_: e.tensor_copy(out=b_[:, :], in_=f_[:, :]), reads=[f_], writes=[b_])
                    S.dma(dst[r0:r0 + 128, c0:c0 + 2048], b_[:, :], reads=[b_], writes=[dst])
                for h in range(8):
                    psP = P2[gi % 3]
                    E = Es[gi % 3]
                    Gm = Gms[gi % 3]
                    gi += 1
                    def mmp(e, psP=psP, h=h, eg=eg, qTt=qTt, PKf=PKf, K2f=K2f):
                        for hf in range(2):
                            c0 = eg * 1024 + hf * 512
                            e.matmul(psP[:, hf * 512:(hf + 1) * 512], lhsT=qTt[:, 2 * h, :], rhs=PKf[:, c0:c0 + 512], start=True, stop=False)
                            r = e.matmul(psP[:, hf * 512:(hf + 1) * 512], lhsT=qTt[:, 2 * h + 1, :], rhs=K2f, start=False, stop=True)
                        return r
                    S.op("pe", mmp, reads=[qTt, PK1T, K2rep], writes=[psP])
                    S.op("act", lambda e, psP=psP, E=E, h=h, tz=tz: e.activation(out=E[:, :], in_=psP[:, :], func=AF.Exp, bias=tz[:, 16 + h:17 + h]), reads=[psP, tz], writes=[E])
                    if h == 0:
                        S.op("dve", lambda e, psP=psP, E=E, Gt=Gt, h=h, tz=tz: e.scalar_tensor_tensor(out=Gt[:, :], in0=psP[:, :], scalar=tz[:, h:h + 1], in1=E[:, :], op0=ALU.is_ge, op1=ALU.mult),
                             reads=[psP, E, tz, Gt], writes=[Gt])
                    else:
                        S.op("dve", lambda e, psP=psP, E=E, Gm=Gm, h=h, tz=tz: e.scalar_tensor_tensor(out=Gm[:, :], in0=psP[:, :], scalar=tz[:, h:h + 1], in1=E[:, :], op0=ALU.is_ge, op1=ALU.mult),
                             reads=[psP, E, tz], writes=[Gm])
                        S.op("dve", lambda e, Gm=Gm, Gt=Gt: e.tensor_tensor(out=Gt[:, :], in0=Gt[:, :], in1=Gm[:, :], op=ALU.add), reads=[Gm, Gt], writes=[Gt])
                S.dma(Gd[tt * 128:(tt + 1) * 128, eg * 1024:(eg + 1) * 1024], Gt[:, :], reads=[Gt], writes=[Gd], q="act")
        assert cv_i == len(cv_steps)
        dbg_dram("Gd", Gd, [NTOK, 16384], BF16)
        if stop_after <= 5:
            return early_exit()
        S.pop()

        S.mark("5B")
        S.push()
        PSF, PBF = psum_std(6, 2)
        psf_rr = RR(PSF)
        actT_d = S.dram("actT_d", [16384, NTOK], BF16)
        xTs = [S.sb([128, 16, 512], BF16, name="xTs%d" % i) for i in range(2)]
        uts = [S.sb([128, 16, 512], BF16, name="ut%d" % i) for i in range(2)]
        gls = [S.sb([128, 512], BF16, name="gl%d" % i) for i in range(3)]
        gts = [S.sb([128, 512], BF16, name="gtl%d" % i) for i in range(4)]
        aTs = [S.sb([128, 4, 128], BF16, name="aT%d" % i) for i in range(3)]
        uT3 = uT_b[:, :].rearrange("(k p) c -> p k c", p=128)
        aT3 = actT_d[:, :].rearrange("(c p) t -> p c t", p=128)
        gi = 0
        for stile in range(4):
            xT = xTs[stile % 2]
            S.dma(xT[:, :, :], x1T_d[:, :].rearrange("(k p) t -> p k t", p=128)[:, :, stile * 512:(stile + 1) * 512], reads=[x1T_d], writes=[xT])
            for ec in range(32):
                ut = uts[ec % 2]
                S.dma(ut[:, :, :], uT3[:, :, ec * 512:(ec + 1) * 512], reads=[uT_b], writes=[ut])
                for sub in range(4):
                    tok0 = stile * 512 + sub * 128
                    gt_ = gts[gi % 4]
                    gl = gls[gi % 3]
                    aT = aTs[gi % 3]
                    pb = PBF[gi % 2]
                    gi += 1
                    S.dma(gt_[:, :], Gd[tok0:tok0 + 128, ec * 512:(ec + 1) * 512], reads=[Gd], writes=[gt_])
                    ps = psf_rr()
                    def mmh(e, ps=ps, xT=xT, ut=ut, sub=sub):
                        for k in range(16):
                            r = e.matmul(ps[:, :], lhsT=xT[:, k, sub * 128:(sub + 1) * 128], rhs=ut[:, k, :], start=(k == 0), stop=(k == 15))
                        return r
                    S.op("pe", mmh, reads=[xT, ut], writes=[ps])
                    S.op("act", lambda e, ps=ps, gl=gl: e.activation(out=gl[:, :], in_=ps[:, :], func=AF.Gelu), reads=[ps], writes=[gl])
                    S.op("dve", lambda e, gl=gl, gt_=gt_: e.tensor_tensor(out=gl[:, :], in0=gl[:, :], in1=gt_[:, :], op=ALU.mult), reads=[gl, gt_], writes=[gl])
                    def tra(e, gl=gl, pb=pb):
                        for j in range(4):
                            r = e.transpose(out=pb[:, j * 128:(j + 1) * 128], in_=gl[:, j * 128:(j + 1) * 128], identity=identb[:, :])
                        return r
                    S.op("pe", tra, reads=[gl, identb], writes=[pb])
                    S.op("dve", lambda e, aT=aT, pb=pb: e.tensor_copy(out=aT[:, :, :].rearrange("p a b -> p (a b)"), in_=pb[:, 0:512]), reads=[pb], writes=[aT])
                    S.dma(aT3[:, ec * 4:(ec + 1) * 4, tok0:tok0 + 128], aT[:, :, :], reads=[aT], writes=[actT_d], q="act")
        if stop_after <= 6:
            return early_exit()
        S.pop()

        S.mark("5C")
        S.push()
        PS8 = [S.ps([128, 512], F32) for i in range(8)]
        gB2 = S.sb([128, 2048], F32, name="gB2")
        bB2 = S.sb([128, 2048], F32, name="bB2")
        S.dma(gB2[:, :], I["ln2_g"].to_broadcast([128, 2048]), writes=[gB2])
        S.dma(bB2[:, :], I["ln2_b"].to_broadcast([128, 2048]), writes=[bB2])
        vvs = [S.sb([128, 2048], BF16, name="vv%d" % i) for i in range(4)]
        ats = [S.sb([128, 256], BF16, name="at%d" % i) for i in range(4)]
        x1s = [S.sb([128, 2048], F32, name="x1s%d" % i) for i in range(2)]
        y2s = [S.sb([128, 2048], F32, name="y2s%d" % i) for i in range(2)]
        jk2 = S.sb([128, 2048], F32, name="jk2")
        st82 = [S.sb([128, 8], F32, name="st82_%d" % i) for i in range(2)]
        for tg in range(8):
            for ec in range(128):
                vv = vvs[ec % 4]
                at = ats[ec % 4]
                S.dma(vv[:, :], ev_b[ec * 128:(ec + 1) * 128, :], reads=[ev_b], writes=[vv])
                S.dma(at[:, :], actT_d[ec * 128:(ec + 1) * 128, tg * 256:(tg + 1) * 256], reads=[actT_d], writes=[at])
                def mmv(e, vv=vv, at=at, ec=ec, PS8=PS8):
                    for hf in range(2):
                        for j in range(4):
                            r = e.matmul(PS8[hf * 4 + j][:, :], lhsT=at[:, hf * 128:(hf + 1) * 128], rhs=vv[:, j * 512:(j + 1) * 512], start=(ec == 0), stop=(ec == 127))
                    return r
                S.op("pe", mmv, reads=[vv, at], writes=PS8)
            for hf in range(2):
                tt = tg * 2 + hf
                x1t, y2, st8 = x1s[hf], y2s[hf], st82[hf]
                S.dma(x1t[:, :], x1_d[tt * 128:(tt + 1) * 128, :], reads=[x1_d], writes=[x1t])
                for j in range(4):
                    pj = PS8[hf * 4 + j]
                    S.op("dve", lambda e, j=j, y2=y2, x1t=x1t, pj=pj: e.scalar_tensor_tensor(out=y2[:, j * 512:(j + 1) * 512], in0=x1t[:, j * 512:(j + 1) * 512], scalar=ALPHA, in1=pj[:, :], op0=ALU.mult, op1=ALU.add),
                         reads=[pj, x1t, y2], writes=[y2])
                layer_norm_tile(y2, gB2, bB2, x1t, st8, jk2)
                S.dma(out_ap[tt * 128:(tt + 1) * 128, :], x1t[:, :], reads=[x1t], writes=[OUT], q="act")
        S.mark("end")
        nc._marks = S.marks
        S.finish([OUT] + dbg_bufs)
        S.emit()
        S.pop_all()
        return nc


def make_in_maps(inp):
    x = np.asarray(inp["x"], np.float32)
    shared = {
        "w_in": np.ascontiguousarray(inp["w_in"][0]),
        "conv_wT": np.ascontiguousarray(inp["conv_w"][0].T.reshape(16, 128, 4).transpose(1, 0, 2)),
        "conv_b": np.ascontiguousarray(inp["conv_b"][0].reshape(16, 128).T),
        "b_ig": np.ascontiguousarray(inp["b_igate"][0].reshape(4, 1)),
        "b_fg": np.ascontiguousarray(inp["b_fgate"][0].reshape(4, 1)),
        "mh_g": np.ascontiguousarray(inp["mh_norm_g"][0].reshape(1, 1024)),
        "w_out": np.ascontiguousarray(inp["w_out"][0]),
        "ln1_g": np.ascontiguousarray(inp["ln1_g"][0].reshape(1, D)),
        "ln1_b": np.ascontiguousarray(inp["ln1_b"][0].reshape(1, D)),
        "w_q": np.ascontiguousarray(inp["w_query"][0]),
        "k1T": np.ascontiguousarray(inp["sub_keys_1"][0].T),
        "k2T": np.ascontiguousarray(inp["sub_keys_2"][0].T),
        "uT": np.ascontiguousarray(inp["expert_u"][0].T),
        "ev": np.ascontiguousarray(inp["expert_v"][0]),
        "ln2_g": np.ascontiguousarray(inp["ln2_g"][0].reshape(1, D)),
        "ln2_b": np.ascontiguousarray(inp["ln2_b"][0].reshape(1, D)),
        "ident": np.eye(128, dtype=np.float32),
    }
    shared = {k: v.astype(np.float32, copy=False) for k, v in shared.items()}
    jj = np.arange(128)[:, None]
    cc = np.arange(256)[None, :]
    maskA = np.where((cc >= jj) & (cc <= jj + 128), 0.0, NEG).astype(np.float32)
    maskC0 = maskA.copy()
    maskC0[:, 0:128] = NEG
    ss = np.arange(64)[:, None]
    tt = np.arange(64)[None, :]
    cmask = (tt >= ss).astype(np.float32)
    shared["maskA"] = maskA
    shared["cmask"] = cmask
    half = 16
    inv = (500000.0 ** (-np.arange(half, dtype=np.float32) / half)).astype(np.float32)
    maps = []
    for c in range(8):
        b, h = c // 2, c % 2
        m = dict(shared)
        xT = np.zeros((D, NALL), np.float32)
        own = x[b, h * 2048:(h + 1) * 2048]
        xT[:, 2048:] = own.T
        if h == 1:
            xT[:, :2048] = x[b, 0:2048].T
        m["xT"] = xT
        m["xown"] = np.ascontiguousarray(own)
        m["flag"] = np.full((128, 1), float(h), np.float32)
        pos = (np.arange(NALL, dtype=np.float32) + (h - 1) * 2048.0).astype(np.float32)
        ang = pos[:, None] * inv[None, :]
        m["cs"] = np.concatenate([np.cos(ang), np.sin(ang)], axis=1).astype(np.float32)
        m["maskC"] = maskA if h == 1 else maskC0
        maps.append(m)
    return maps


_NC_CACHE = {}


def kernel(**inputs):
    maps = make_in_maps(inputs)
    if "nc" not in _NC_CACHE:
        _NC_CACHE["nc"] = build_program()
    nc = _NC_CACHE["nc"]
    res = run_bass_kernel_spmd(nc, maps, core_ids=list(range(8)))
    out = np.zeros((4, 4096, D), np.float32)
    for c in range(8):
        b, h = c // 2, c % 2
        out[b, h * 2048:(h + 1) * 2048] = res.results[c]["out"]
    return out
```
